# Optimizing a Trainium2 kernel written in Bass

```python
import jax, jax.numpy as jnp
from jax import lax
import numpy as np

D_MODEL = 2048
BATCH = 4
SEQ = 4096
DEPTH = 1

Q_BLOCK = 128
HEAD_WIDTH = 128
MLA_HEADS = (D_MODEL // 2) // HEAD_WIDTH
MLA_NOPE_DIM = 128
MLA_ROPE_DIM = 64
MLA_V_DIM = 128
MLA_QK_DIM = MLA_NOPE_DIM + MLA_ROPE_DIM
MLA_Q_RANK = 768
MLA_KV_RANK = 512
MLA_WIDTH = MLA_HEADS * MLA_V_DIM
FOX_HEADS = (D_MODEL // 2) // HEAD_WIDTH
FOX_HEAD_DIM = HEAD_WIDTH
FOX_WIDTH = FOX_HEADS * FOX_HEAD_DIM
D_MIX = MLA_WIDTH + FOX_WIDTH
ROPE_THETA = 10000.0
NORM_EPS = 1e-6
IN_SPLITS = (MLA_Q_RANK, MLA_KV_RANK, MLA_ROPE_DIM, MLA_WIDTH,
             FOX_WIDTH, FOX_WIDTH, FOX_WIDTH, FOX_HEADS, FOX_WIDTH)
D_IN = (MLA_Q_RANK + MLA_KV_RANK + MLA_ROPE_DIM + MLA_WIDTH
        + 4 * FOX_WIDTH + FOX_HEADS)

kernel_name = "hybrid_mla_fox_parallel_heads"


def _rms_norm(x, g):
    xf = x.astype(jnp.float32)
    y = xf * lax.rsqrt(jnp.mean(xf * xf, axis=-1, keepdims=True) + NORM_EPS)
    return (y * g.astype(jnp.float32)).astype(x.dtype)


def _rope_angles(positions, dim):
    inv_freq = ROPE_THETA ** (-jnp.arange(0, dim, 2, dtype=jnp.float32) / dim)
    ang = positions.astype(jnp.float32)[..., None] * inv_freq
    return jnp.cos(ang), jnp.sin(ang)


def _apply_rope(x, cos, sin):
    xf = x.astype(jnp.float32)
    half = xf.shape[-1] // 2
    x1, x2 = xf[..., :half], xf[..., half:]
    out = jnp.concatenate([x1 * cos - x2 * sin, x2 * cos + x1 * sin], axis=-1)
    return out.astype(x.dtype)


def _causal_block_sweep(score_fn, v):
    b, seq = v.shape[0], v.shape[1]
    n_blocks = seq // Q_BLOCK
    key_pos = jnp.arange(seq)

    def one_block(i):
        start = i * Q_BLOCK
        logits = score_fn(start)
        q_pos = start + jnp.arange(Q_BLOCK)
        logits = jnp.where(key_pos[None, :] <= q_pos[:, None], logits, -jnp.inf)
        p = jax.nn.softmax(logits, axis=-1).astype(v.dtype)
        return jnp.einsum('bhqs,bshd->bqhd', p, v)

    out = lax.map(one_block, jnp.arange(n_blocks))
    return out.transpose(1, 0, 2, 3, 4).reshape(b, seq, -1)


def _mla_branch(q_lat, kv_lat, k_rope_raw, g_q, w_uq, g_kv, w_ukv, cos, sin):
    b, s, _ = q_lat.shape
    q = (_rms_norm(q_lat, g_q) @ w_uq).reshape(b, s, MLA_HEADS, MLA_QK_DIM)
    q_nope = q[..., :MLA_NOPE_DIM]
    q_rope = _apply_rope(q[..., MLA_NOPE_DIM:], cos[:, :, None, :], sin[:, :, None, :])
    kv = (_rms_norm(kv_lat, g_kv) @ w_ukv).reshape(b, s, MLA_HEADS, MLA_NOPE_DIM + MLA_V_DIM)
    k_nope, v = kv[..., :MLA_NOPE_DIM], kv[..., MLA_NOPE_DIM:]
    k_rope = _apply_rope(k_rope_raw, cos, sin)
    scale = MLA_QK_DIM ** -0.5

    def score(start):
        qn = lax.dynamic_slice_in_dim(q_nope, start, Q_BLOCK, axis=1)
        qr = lax.dynamic_slice_in_dim(q_rope, start, Q_BLOCK, axis=1)
        s_nope = jnp.einsum('bqhd,bshd->bhqs', qn, k_nope, preferred_element_type=jnp.float32)
        s_rope = jnp.einsum('bqhr,bsr->bhqs', qr, k_rope, preferred_element_type=jnp.float32)
        return (s_nope + s_rope) * scale

    return _causal_block_sweep(score, v)


def _fox_branch(q, k, v, f_logit, b_forget):
    b, s, _ = q.shape
    q = q.reshape(b, s, FOX_HEADS, FOX_HEAD_DIM)
    k = k.reshape(b, s, FOX_HEADS, FOX_HEAD_DIM)
    v = v.reshape(b, s, FOX_HEADS, FOX_HEAD_DIM)
    log_f = jax.nn.log_sigmoid(f_logit.astype(jnp.float32) + b_forget.astype(jnp.float32))
    c = jnp.cumsum(log_f, axis=1).transpose(0, 2, 1)
    scale = FOX_HEAD_DIM ** -0.5

    def score(start):
        qb = lax.dynamic_slice_in_dim(q, start, Q_BLOCK, axis=1)
        cq = lax.dynamic_slice_in_dim(c, start, Q_BLOCK, axis=2)
        logits = jnp.einsum('bqhd,bshd->bhqs', qb, k, preferred_element_type=jnp.float32) * scale
        return logits + cq[:, :, :, None] - c[:, :, None, :]

    return _causal_block_sweep(score, v)


def _hybrid_layer(x, cos, sin, g_pre, w_in, g_q, w_uq, g_kv, w_ukv, b_forget, w_out, g_post):
    h = _rms_norm(x, g_pre)
    proj = h @ w_in
    split_points = np.cumsum(IN_SPLITS)[:-1].tolist()
    (q_lat, kv_lat, k_rope_raw, gate_mla,
     fq, fk, fv, f_logit, gate_fox) = jnp.split(proj, split_points, axis=-1)
    o_mla = _mla_branch(q_lat, kv_lat, k_rope_raw, g_q, w_uq, g_kv, w_ukv, cos, sin) * jax.nn.silu(gate_mla)
    o_fox = _fox_branch(fq, fk, fv, f_logit, b_forget) * jax.nn.silu(gate_fox)
    o = jnp.concatenate([o_mla, o_fox], axis=-1) @ w_out
    return x + _rms_norm(o, g_post)


def setup_inputs(seed: int = 0) -> dict:
    key = jax.random.key(seed)
    ks = jax.random.split(key, 12)
    f32 = jnp.float32
    x = jax.random.normal(ks[0], (BATCH, SEQ, D_MODEL), f32)
    offsets = jax.random.randint(ks[1], (BATCH, 1), 0, 1024, dtype=jnp.int32)
    positions = jnp.arange(SEQ, dtype=jnp.int32)[None, :] + offsets
    g_pre = 1.0 + 0.02 * jax.random.normal(ks[2], (DEPTH, D_MODEL), f32)
    w_in = jax.random.normal(ks[3], (DEPTH, D_MODEL, D_IN), f32) * D_MODEL ** -0.5
    g_q_latent = 1.0 + 0.02 * jax.random.normal(ks[4], (DEPTH, MLA_Q_RANK), f32)
    w_uq = jax.random.normal(ks[5], (DEPTH, MLA_Q_RANK, MLA_HEADS * MLA_QK_DIM), f32) * MLA_Q_RANK ** -0.5
    g_kv_latent = 1.0 + 0.02 * jax.random.normal(ks[6], (DEPTH, MLA_KV_RANK), f32)
    w_ukv = jax.random.normal(ks[7], (DEPTH, MLA_KV_RANK, MLA_HEADS * (MLA_NOPE_DIM + MLA_V_DIM)), f32) * MLA_KV_RANK ** -0.5
    b_forget = 3.0 + 0.1 * jax.random.normal(ks[8], (DEPTH, FOX_HEADS), f32)
    w_out = jax.random.normal(ks[9], (DEPTH, D_MIX, D_MODEL), f32) * D_MIX ** -0.5
    g_post = 1.0 + 0.02 * jax.random.normal(ks[10], (DEPTH, D_MODEL), f32)
    return {"x": x, "positions": positions, "g_pre": g_pre, "w_in": w_in,
            "g_q_latent": g_q_latent, "w_uq": w_uq, "g_kv_latent": g_kv_latent,
            "w_ukv": w_ukv, "b_forget": b_forget, "w_out": w_out, "g_post": g_post}


def reference(x, positions, g_pre, w_in, g_q_latent, w_uq, g_kv_latent, w_ukv, b_forget, w_out, g_post):
    cos, sin = _rope_angles(positions, MLA_ROPE_DIM)
    for l in range(DEPTH):
        x = _hybrid_layer(x, cos, sin, g_pre[l], w_in[l], g_q_latent[l], w_uq[l],
                          g_kv_latent[l], w_ukv[l], b_forget[l], w_out[l], g_post[l])
    return x
```

```python
import os
import contextlib
import numpy as np
import concourse.bass as bass
import concourse.mybir as mybir
from concourse.bass_utils import run_bass_kernel_spmd

F32 = mybir.dt.float32
BF16 = mybir.dt.bfloat16
I32 = mybir.dt.int32
AF = mybir.ActivationFunctionType
ALU = mybir.AluOpType

D = 2048
SEQ = 4096
NBLK = 32
D_IN = 6472
EPS = 1e-6
NEG = -30000.0
SCALE_MLA = 192 ** -0.5
SCALE_FOX = 128 ** -0.5
TWO_PI = 2.0 * np.pi
SIN_SCALE = TWO_PI * (1.0 - 1e-6)
C_QLAT, C_KVLAT, C_KROPE, C_GMLA, C_FQ, C_FK, C_FV, C_FLOG, C_GFOX = 0, 768, 1280, 1344, 2368, 3392, 4416, 5440, 5448

DEBUG = bool(int(os.environ.get("MK_DEBUG", "0")))
STOP_AFTER = os.environ.get("MK_STOP", "")
OLDCQ = bool(int(os.environ.get("MK_OLDCQ", "0")))


class Buf:
    __slots__ = ("w", "r", "name")

    def __init__(self, name=""):
        self.w = None
        self.r = {}
        self.name = name


class T:
    def __init__(self, nc, name, shape, dt, psum=False):
        if psum:
            self.t = nc.alloc_psum_tensor("ps_" + name, shape, dt)
        else:
            self.t = nc.alloc_sbuf_tensor("sb_" + name, shape, dt)
        self.b = Buf(name)


def _b(x):
    return x.b if isinstance(x, T) else x


class EngQ:
    def __init__(self, name):
        self.name = name
        self.ops = []
        self.waited = {}


class Prog:
    ENGS = ["tensor", "vector", "scalar", "gpsimd", "sync"]
    SEM_LIMIT = 28000

    def __init__(self, nc):
        self.nc = nc
        self.q = {n: EngQ(n) for n in self.ENGS}
        self.semcount = {}
        self.semkeys = []
        self.cur = {}
        self.gen = {}
        for n in self.ENGS:
            self.gen[n] = 0
            self.cur[n] = "E_%s_0" % n
            self._mksem(self.cur[n])

    def _mksem(self, key):
        if key not in self.semcount:
            self.semcount[key] = 0
            self.semkeys.append(key)

    def dma_sem(self, key):
        key = "D_" + key
        self._mksem(key)
        return key

    def op(self, eng, fn, reads=(), writes=(), dma=None):
        q = self.q[eng]
        need = {}
        for b in reads:
            b = _b(b)
            if b.w is not None:
                k, v = b.w
                if need.get(k, 0) < v:
                    need[k] = v
        for b in writes:
            b = _b(b)
            if b.w is not None:
                k, v = b.w
                if need.get(k, 0) < v:
                    need[k] = v
            for k, v in b.r.items():
                if need.get(k, 0) < v:
                    need[k] = v
        own = self.cur[eng]
        for k, v in need.items():
            if eng == "tensor" and dma is None and k.startswith("E_tensor"):
                continue
            if q.waited.get(k, 0) < v:
                q.waited[k] = v
                q.ops.append(("wait", k, v))
        if dma is not None:
            if dma.startswith("D_") is False:
                dma = self.dma_sem(dma)
            if self.semcount[dma] + 16 > self.SEM_LIMIT:
                raise RuntimeError("dma sem overflow " + dma)
            self.semcount[dma] += 16
            ev = (dma, self.semcount[dma])
            q.ops.append(("op", fn, dma, 16))
        else:
            if self.semcount[own] + 1 > self.SEM_LIMIT:
                self.gen[eng] += 1
                own = "E_%s_%d" % (eng, self.gen[eng])
                self.cur[eng] = own
                self._mksem(own)
            self.semcount[own] += 1
            ev = (own, self.semcount[own])
            q.ops.append(("op", fn, own, 1))
        for b in writes:
            b = _b(b)
            b.w = ev
            b.r = {}
        for b in reads:
            b = _b(b)
            if b.r.get(ev[0], 0) < ev[1]:
                b.r[ev[0]] = ev[1]
        return ev

    def barrier(self):
        for n in self.ENGS:
            q = self.q[n]
            for k in self.semkeys:
                v = self.semcount[k]
                if v > 0 and q.waited.get(k, 0) < v:
                    q.waited[k] = v
                    q.ops.append(("wait", k, v))

    def emit(self):
        nc = self.nc
        self.barrier()
        with contextlib.ExitStack() as es:
            sems = {}
            for k in self.semkeys:
                sems[k] = es.enter_context(nc.semaphore(k))
            block = es.enter_context(nc.Block())

            def make(engname):
                def body(e):
                    for item in self.q[engname].ops:
                        if item[0] == "wait":
                            e.wait_ge(sems[item[1]], item[2])
                        else:
                            _, fn, k, inc = item
                            fn(e).then_inc(sems[k], inc)
                return body
            block.tensor(make("tensor"))
            block.vector(make("vector"))
            block.scalar(make("scalar"))
            block.gpsimd(make("gpsimd"))
            block.sync(make("sync"))


class Rot:
    def __init__(self, tiles):
        self.tiles = tiles
        self.i = 0

    def next(self):
        t = self.tiles[self.i % len(self.tiles)]
        self.i += 1
        return t


def build_program():
    nc = bass.Bass("TRN2", target_bir_lowering=False)
    P = Prog(nc)
    op = P.op

    def dsem(t):
        return P.dma_sem('w_' + t.b.name)

    def din(name, shape, dt=F32):
        return nc.dram_tensor(name, shape, dt, kind="ExternalInput").ap()

    def dscr(name, shape, dt=BF16):
        if DEBUG:
            return nc.dram_tensor(name, shape, dt, kind="ExternalOutput").ap()
        return nc.dram_tensor(name, shape, dt).ap()

    x_d = din("x", [SEQ, D])
    pos_d = din("pos", [1, SEQ], I32)
    gpre_d = din("g_pre", [1, D])
    gpost_d = din("g_post", [1, D])
    gq_d = din("gq_col", [128, 6])
    gkv_d = din("gkv_col", [128, 4])
    bf_d = din("b_forget", [1, 8])
    cst_d = din("cst", [64, 4])
    flag_d = din("flag", [1, 2])
    mpred_d = din("mpred", [32, 32])
    w_in_d = din("w_in", [D, D_IN])
    w_uq_d = din("w_uq", [768, 1536])
    w_uqsw_d = din("w_uq_sw", [768, 512])
    w_uk_d = din("w_uk", [512, 1024])
    w_uv_d = din("w_uv", [512, 1024])
    w_kr2_d = din("w_kr2", [D, 128])
    w_out_d = din("w_out", [D, D])
    out_d = nc.dram_tensor("out", [2048, D], F32, kind="ExternalOutput").ap()

    QT_d = dscr("QT", [8, 192, 2048])
    KT_d = dscr("KT", [8, 128, SEQ])
    KR_d = dscr("KR", [64, SEQ])
    VM_d = dscr("VM", [8, 128, NBLK, 128])
    FQT_d = dscr("FQT", [8, 128, 2048])
    FKT_d = dscr("FKT", [8, 128, SEQ])
    VF_d = dscr("VF", [8, 128, NBLK, 128])
    GT_d = dscr("GT", [16, 128, 2048])
    CCT_d = nc.dram_tensor("CCT", [128, 128], F32).ap()
    if DEBUG:
        LOGF_d = dscr("LOGF_dbg", [128, 256], F32)
        CC_d = dscr("CC_dbg", [128, 256], F32)

    w_in_v = w_in_d.rearrange("(kc p) n -> p kc n", p=128)

    ident_f = T(nc, "ident_f", [128, 128], F32)
    ident_bf = T(nc, "ident_bf", [128, 128], BF16)
    ones_bf = T(nc, "ones_bf", [128, 128], BF16)
    ones_f = T(nc, "ones_f", [128, 128], F32)
    utri_f = T(nc, "utri_f", [128, 128], F32)
    trimask = T(nc, "trimask", [128, 128], F32)
    LOGF = T(nc, "LOGF", [128, 256], F32)
    CC = T(nc, "CC", [128, 256], F32)
    NEGC = T(nc, "NEGC", [128, 256], F32)
    NEGCF = T(nc, "NEGCF", [128, 128], F32)
    flag_bc = T(nc, "flag_bc", [128, 2], F32)
    flag01 = T(nc, "flag01", [128, 2], F32)
    tri01 = T(nc, "tri01", [128, 128], BF16)

    ld_misc = P.dma_sem("ld_misc")
    op("gpsimd", lambda e: e.memset(ident_f.t[:], 0.0), writes=[ident_f])
    op("gpsimd", lambda e: e.affine_select(out=ident_f.t[:], in_=ident_f.t[:], pattern=[[-1, 128]],
                                           compare_op=ALU.not_equal, fill=1.0, base=0, channel_multiplier=1),
       reads=[ident_f], writes=[ident_f])
    op("vector", lambda e: e.tensor_copy(out=ident_bf.t[:], in_=ident_f.t[:]), reads=[ident_f], writes=[ident_bf])
    op("gpsimd", lambda e: e.memset(ones_bf.t[:], 1.0), writes=[ones_bf])
    op("gpsimd", lambda e: e.memset(ones_f.t[:], 1.0), writes=[ones_f])
    op("gpsimd", lambda e: e.memset(utri_f.t[:], 1.0), writes=[utri_f])
    op("gpsimd", lambda e: e.affine_select(out=utri_f.t[:], in_=utri_f.t[:], pattern=[[1, 128]],
                                           compare_op=ALU.is_ge, fill=0.0, base=0, channel_multiplier=-1),
       reads=[utri_f], writes=[utri_f])
    op("gpsimd", lambda e: e.memset(trimask.t[:], 0.0), writes=[trimask])
    op("gpsimd", lambda e: e.affine_select(out=trimask.t[:], in_=trimask.t[:], pattern=[[1, 128]],
                                           compare_op=ALU.is_ge, fill=NEG, base=0, channel_multiplier=-1),
       reads=[trimask], writes=[trimask])
    op("sync", lambda e: e.dma_start(out=flag_bc.t[:], in_=flag_d.partition_broadcast(128)), writes=[flag_bc], dma=dsem(flag_bc))
    op("vector", lambda e: e.tensor_scalar(out=flag01.t[:], in0=flag_bc.t[:], scalar1=1.0 / 30000.0, scalar2=1.0, op0=ALU.mult, op1=ALU.add),
       reads=[flag_bc], writes=[flag01])
    op("vector", lambda e: e.tensor_copy(out=tri01.t[:], in_=utri_f.t[:]), reads=[utri_f], writes=[tri01])

    esPA = contextlib.ExitStack()

    def psum_tile(es, name, shape, dt):
        t = T.__new__(T)
        t.t = es.enter_context(nc.psum_tensor("ps_" + name, shape, dt))
        t.b = Buf(name)
        return t
    psT = [psum_tile(esPA, "psT%d" % i, [128, 8, 128], BF16) for i in range(2)]
    pbank = [psum_tile(esPA, "pbank%d" % i, [128, 512], F32) for i in range(6)]

    esA = contextlib.ExitStack()

    def sbA(name, shape, dt):
        t = T.__new__(T)
        t.t = esA.enter_context(nc.sbuf_tensor("sa_" + name, shape, dt))
        t.b = Buf(name)
        return t

    gpre_bc = sbA("gpre_bc", [128, D], F32)
    gq_col = sbA("gq_col", [128, 6], F32)
    gkv_col = sbA("gkv_col", [128, 4], F32)
    bf_bc = sbA("bf_bc", [128, 8], F32)
    cst = sbA("cst", [64, 4], F32)
    wuq = sbA("wuq", [128, 6, 1536], BF16)
    wuqsw = sbA("wuqsw", [128, 6, 512], BF16)
    wuk = sbA("wuk", [128, 4, 1024], BF16)
    wuv = sbA("wuv", [128, 4, 1024], BF16)
    wkr2 = sbA("wkr2", [128, 16, 128], BF16)
    wfl = sbA("wfl", [128, 16, 8], BF16)
    hT = sbA("hT", [128, 16, 1024], BF16)
    NWB = 2
    wbufs = Rot([sbA("wbuf%d" % i, [128, 16, 512], BF16) for i in range(NWB)])
    xs = [sbA("xs%d" % i, [128, D], F32) for i in range(2)]
    hb = [sbA("hb%d" % i, [128, D], BF16) for i in range(2)]
    ssx = sbA("ssx", [128, 32], F32)
    rsx = sbA("rsx", [128, 32], F32)
    latq = sbA("latq", [128, 6, 1024], BF16)
    latkv = sbA("latkv", [128, 4, 1024], BF16)
    sqt = Rot([sbA("sq%d" % i, [128, 512], BF16) for i in range(2)])
    rstd_bc = sbA("rstd_bc", [128, 1024], F32)
    posi = sbA("posi", [64, 512], I32)
    posf = sbA("posf", [64, 512], F32)
    ru = sbA("ru", [64, 512], F32)
    rkf = sbA("rkf", [64, 512], F32)
    C2 = sbA("C2", [64, 1024], F32)
    S2 = sbA("S2", [64, 1024], F32)
    ropeA = Rot([sbA("ropeA%d" % i, [64, 512], F32) for i in range(1)])
    ropeB = Rot([sbA("ropeB%d" % i, [64, 512], F32) for i in range(1)])
    stg = Rot([sbA("stg%d" % i, [128, 512], BF16) for i in range(4)])
    stg_sem = [P.dma_sem("stg%d" % i) for i in range(4)]
    vstM = sbA("vst", [128, 8, 512], BF16)
    vstF = vstM
    zt = sbA("zt", [128, 8], F32)
    zt2 = sbA("zt2", [128, 8], F32)
    zt3 = sbA("zt3", [128, 8], F32)

    banks = Rot(pbank[0:4])
    banks6 = Rot(pbank[0:6])
    ssb = [pbank[4], pbank[5]]

    ldw = P.dma_sem("ldw_res")
    op("sync", lambda e: e.dma_start(out=gpre_bc.t[:], in_=gpre_d.partition_broadcast(128)), writes=[gpre_bc], dma=dsem(gpre_bc))
    op("sync", lambda e: e.dma_start(out=gq_col.t[:], in_=gq_d), writes=[gq_col], dma=dsem(gq_col))
    op("sync", lambda e: e.dma_start(out=gkv_col.t[:], in_=gkv_d), writes=[gkv_col], dma=dsem(gkv_col))
    op("sync", lambda e: e.dma_start(out=bf_bc.t[:], in_=bf_d.partition_broadcast(128)), writes=[bf_bc], dma=dsem(bf_bc))
    op("sync", lambda e: e.dma_start(out=cst.t[:], in_=cst_d), writes=[cst], dma=dsem(cst))
    op("gpsimd", lambda e: e.memset(ssx.t[:], 0.0), writes=[ssx])
    op("gpsimd", lambda e: e.dma_start(out=wkr2.t[:], in_=w_kr2_d.rearrange("(kc p) n -> p kc n", p=128)), writes=[wkr2], dma=dsem(wkr2))
    op("gpsimd", lambda e: e.dma_start(out=wfl.t[:], in_=w_in_v[:, :, C_FLOG:C_FLOG + 8]), writes=[wfl], dma=dsem(wfl))

    def load_resident_weights():
        op("gpsimd", lambda e: e.dma_start(out=wuq.t[:], in_=w_uq_d.rearrange("(kc p) n -> p kc n", p=128)), writes=[wuq], dma=dsem(wuq))
        op("gpsimd", lambda e: e.dma_start(out=wuqsw.t[:], in_=w_uqsw_d.rearrange("(kc p) n -> p kc n", p=128)), writes=[wuqsw], dma=dsem(wuqsw))
        op("gpsimd", lambda e: e.dma_start(out=wuk.t[:], in_=w_uk_d.rearrange("(kc p) n -> p kc n", p=128)), writes=[wuk], dma=dsem(wuk))
        op("gpsimd", lambda e: e.dma_start(out=wuv.t[:], in_=w_uv_d.rearrange("(kc p) n -> p kc n", p=128)), writes=[wuv], dma=dsem(wuv))

    x_sem = [P.dma_sem("ldx%d" % i) for i in range(2)]

    def prepL(sg, blk):
        gb = sg * 8 + blk
        bi = gb % 2
        r0 = gb * 128
        op("gpsimd", lambda e: e.dma_start(out=xs[bi].t[:], in_=x_d[r0:r0 + 128, :]), writes=[xs[bi]], dma=x_sem[bi])

    def prepA(sg, blk):
        gb = sg * 8 + blk
        bi = gb % 2
        op("scalar", lambda e: e.activation(out=hb[bi].t[:], in_=xs[bi].t[:], func=AF.Square, accum_out=ssx.t[:, gb:gb + 1]),
           reads=[xs[bi]], writes=[hb[bi], ssx])
        op("vector", lambda e: e.tensor_scalar(out=rsx.t[:, gb:gb + 1], in0=ssx.t[:, gb:gb + 1], scalar1=1.0 / D, scalar2=EPS,
                                               op0=ALU.mult, op1=ALU.add), reads=[ssx], writes=[rsx])
        op("scalar", lambda e: e.activation(out=rsx.t[:, gb:gb + 1], in_=rsx.t[:, gb:gb + 1], func=AF.Sqrt), reads=[rsx], writes=[rsx])
        op("vector", lambda e: e.reciprocal(out=rsx.t[:, gb:gb + 1], in_=rsx.t[:, gb:gb + 1]), reads=[rsx], writes=[rsx])
        op("vector", lambda e: e.scalar_tensor_tensor(out=hb[bi].t[:], in0=xs[bi].t[:], scalar=rsx.t[:, gb:gb + 1], in1=gpre_bc.t[:],
                                                      op0=ALU.mult, op1=ALU.mult), reads=[xs[bi], rsx, gpre_bc], writes=[hb[bi]])

    def prepB(sg, blk):
        gb = sg * 8 + blk
        bi = gb % 2
        for kc in range(16):
            pt = psT[kc // 8]
            op("tensor", lambda e, kc=kc, pt=pt: e.transpose(out=pt.t[:, kc % 8, :], in_=hb[bi].t[:, kc * 128:(kc + 1) * 128],
                                                             identity=ident_bf.t[:]), reads=[hb[bi], ident_bf], writes=[pt])
        op("scalar", lambda e: e.activation(out=hT.t[:, 0:8, blk * 128:(blk + 1) * 128], in_=psT[0].t[:], func=AF.Copy),
           reads=[psT[0]], writes=[hT])
        op("vector", lambda e: e.tensor_copy(out=hT.t[:, 8:16, blk * 128:(blk + 1) * 128], in_=psT[1].t[:]),
           reads=[psT[1]], writes=[hT])

    def rope_tables(sg):
        for hf in range(2):
            t0 = sg * 1024 + hf * 512
            hs = slice(hf * 512, (hf + 1) * 512)
            op("sync", lambda e, t0=t0: e.dma_start(out=posi.t[:], in_=pos_d[0:1, t0:t0 + 512].partition_broadcast(64)), writes=[posi], dma=dsem(posi))
            op("vector", lambda e: e.tensor_copy(out=posf.t[:], in_=posi.t[:]), reads=[posi], writes=[posf])
            for col, tab in ((1, S2), (2, C2)):
                op("vector", lambda e, col=col: e.tensor_scalar(out=ru.t[:], in0=posf.t[:], scalar1=cst.t[:, 0:1], scalar2=cst.t[:, col:col + 1],
                                                                op0=ALU.mult, op1=ALU.add), reads=[posf, cst], writes=[ru])
                op("vector", lambda e: e.tensor_copy(out=posi.t[:], in_=ru.t[:]), reads=[ru], writes=[posi])
                op("vector", lambda e: e.tensor_copy(out=rkf.t[:], in_=posi.t[:]), reads=[posi], writes=[rkf])
                op("vector", lambda e: e.tensor_tensor(out=ru.t[:], in0=ru.t[:], in1=rkf.t[:], op=ALU.subtract), reads=[ru, rkf], writes=[ru])
                op("scalar", lambda e, tab=tab, hs=hs: e.activation(out=tab.t[:, hs], in_=ru.t[:], func=AF.Sin, scale=SIN_SCALE), reads=[ru], writes=[tab])

    def stage_out(fn_evac, eng, reads, dst_ap, rows=128):
        i = stg.i % len(stg.tiles)
        st = stg.next()
        op(eng, lambda e: fn_evac(e, st.t[0:rows, :]), reads=reads, writes=[st])
        op("sync", lambda e: e.dma_start(out=dst_ap, in_=st.t[0:rows, :]), reads=[st], dma=stg_sem[i])

    def load_wgroup(c0, ncols):
        wb = wbufs.next()
        op("gpsimd", lambda e: e.dma_start(out=wb.t[:, :, 0:ncols], in_=w_in_v[:, :, c0:c0 + ncols]), writes=[wb],
           dma=P.dma_sem("ldwb%d" % ((wbufs.i - 1) % NWB)))
        return wb

    def fm_matmul(wt, c0, M, half, nk=16, rhs=None, pool=None):
        rhs = rhs or hT
        bank = (pool or banks).next()
        for kc in range(nk):
            op("tensor", lambda e, kc=kc: e.matmul(bank.t[0:M, :], lhsT=wt.t[:, kc, c0:c0 + M], rhs=rhs.t[:, kc, half * 512:(half + 1) * 512],
                                                   start=(kc == 0), stop=(kc == nk - 1)), reads=[wt, rhs], writes=[bank])
        return bank

    def lat_group(wb, nchunks, lat, chunk0, gcol, first, last_total):
        pend = []

        def flush():
            sq, half, cc = pend.pop(0)
            op("tensor", lambda e: e.matmul(ssb[half].t[:], lhsT=ones_bf.t[:], rhs=sq.t[:],
                                            start=(cc == 0), stop=(cc == last_total - 1)),
               reads=[sq, ones_bf], writes=[ssb[half]])
        for c in range(nchunks):
            cc = chunk0 + c
            for half in range(2):
                bank = fm_matmul(wb, c * 128, 128, half)
                op("scalar", lambda e, bank=bank, cc=cc, half=half: e.activation(
                    out=lat.t[:, cc, half * 512:(half + 1) * 512], in_=bank.t[:], func=AF.Copy, scale=gcol.t[:, cc:cc + 1]),
                   reads=[bank, gcol], writes=[lat])
                sq = sqt.next()
                op("scalar", lambda e, bank=bank, sq=sq: e.activation(out=sq.t[:], in_=bank.t[:], func=AF.Square),
                   reads=[bank], writes=[sq])
                if pend:
                    flush()
                pend.append((sq, half, cc))
        while pend:
            flush()

    def lat_normalize(lat, nch, n):
        for half in range(2):
            op("vector", lambda e, half=half: e.tensor_scalar(out=rstd_bc.t[:, half * 512:(half + 1) * 512], in0=ssb[half].t[:],
                                                              scalar1=1.0 / n, scalar2=EPS, op0=ALU.mult, op1=ALU.add),
               reads=[ssb[half]], writes=[rstd_bc])
        op("scalar", lambda e: e.activation(out=rstd_bc.t[:], in_=rstd_bc.t[:], func=AF.Sqrt), reads=[rstd_bc], writes=[rstd_bc])
        op("vector", lambda e: e.reciprocal(out=rstd_bc.t[:], in_=rstd_bc.t[:]), reads=[rstd_bc], writes=[rstd_bc])
        for c in range(nch):
            op("vector", lambda e, c=c: e.tensor_tensor(out=lat.t[:, c, :], in0=lat.t[:, c, :], in1=rstd_bc.t[:], op=ALU.mult),
               reads=[lat, rstd_bc], writes=[lat])

    def rope_evac(bank_x, bank_xs, half, scale, dst_ap):
        ra = ropeA.next()
        rb = ropeB.next()
        hs = slice(half * 512, (half + 1) * 512)
        op("vector", lambda e: e.tensor_tensor(out=ra.t[:], in0=bank_x.t[0:64, :], in1=C2.t[:, hs], op=ALU.mult),
           reads=[bank_x, C2], writes=[ra])
        op("vector", lambda e: e.tensor_tensor(out=rb.t[:], in0=bank_xs.t[0:64, :], in1=S2.t[:, hs], op=ALU.mult),
           reads=[bank_xs, S2], writes=[rb])
        op("vector", lambda e: e.tensor_tensor(out=ra.t[:], in0=ra.t[:], in1=rb.t[:], op=ALU.add), reads=[ra, rb], writes=[ra])
        stage_out(lambda e, o: e.activation(out=o, in_=ra.t[:], func=AF.Copy, scale=scale), "scalar", [ra], dst_ap, rows=64)

    PRELOADED = []

    def process_sg(sg, nsg_total):
        own = sg < 2
        t0 = sg * 1024
        q0 = sg * 1024
        if sg == 0:
            prepL(0, 0)
            prepL(0, 1)
            prepA(0, 0)
            for blk in range(8):
                if blk + 1 < 8:
                    prepA(0, blk + 1)
                if blk + 2 < 8:
                    prepL(0, blk + 2)
                prepB(0, blk)
        rope_tables(sg)
        tasks = []

        def t_qlat0(wb):
            lat_group(wb, 4, latq, 0, gq_col, True, 6)

        def t_qlat1(wb):
            lat_group(wb, 2, latq, 4, gq_col, False, 6)
            lat_normalize(latq, 6, 768.0)

        def qup_head(h):
            for half in range(2):
                tok = slice(q0 + half * 512, q0 + (half + 1) * 512)
                bank = fm_matmul(wuq, h * 192, 128, half, nk=6, rhs=latq, pool=banks6)
                stage_out(lambda e, o, bank=bank: e.activation(out=o, in_=bank.t[:], func=AF.Copy, scale=SCALE_MLA),
                          "scalar", [bank], QT_d[h, 0:128, tok])
                bx = fm_matmul(wuq, h * 192 + 128, 64, half, nk=6, rhs=latq, pool=banks6)
                bxs = fm_matmul(wuqsw, h * 64, 64, half, nk=6, rhs=latq, pool=banks6)
                rope_evac(bx, bxs, half, SCALE_MLA, QT_d[h, 128:192, tok])

        def t_kvlat(wb):
            lat_group(wb, 4, latkv, 0, gkv_col, True, 4)
            lat_normalize(latkv, 4, 512.0)

        def kup_head(h):
            for half in range(2):
                tok = slice(t0 + half * 512, t0 + (half + 1) * 512)
                bank = fm_matmul(wuk, h * 128, 128, half, nk=4, rhs=latkv, pool=banks6)
                stage_out(lambda e, o, bank=bank: e.tensor_copy(out=o, in_=bank.t[:]), "vector", [bank], KT_d[h, :, tok])

        def vup_piece(i):
            cg = i // 4
            for blk in ((i % 4) * 2, (i % 4) * 2 + 1):
                bank = banks6.next()
                for kc in range(4):
                    op("tensor", lambda e, kc=kc, bank=bank, blk=blk, cg=cg: e.matmul(
                        bank.t[:], lhsT=latkv.t[:, kc, blk * 128:(blk + 1) * 128], rhs=wuv.t[:, kc, cg * 512:(cg + 1) * 512],
                        start=(kc == 0), stop=(kc == 3)), reads=[latkv, wuv], writes=[bank])
                if blk % 2 == 0:
                    op("vector", lambda e, bank=bank, blk=blk: e.tensor_copy(out=vstM.t[:, blk, :], in_=bank.t[:]),
                       reads=[bank], writes=[vstM])
                else:
                    op("scalar", lambda e, bank=bank, blk=blk: e.activation(out=vstM.t[:, blk, :], in_=bank.t[:], func=AF.Copy),
                       reads=[bank], writes=[vstM])
            if i % 4 == 3:
                for h4 in range(4):
                    h = cg * 4 + h4
                    op("sync", lambda e, h=h, h4=h4: e.dma_start(out=VM_d[h, :, sg * 8:(sg + 1) * 8, :], in_=vstM.t[:, :, h4 * 128:(h4 + 1) * 128]),
                       reads=[vstM], dma=P.dma_sem("vstM"))

        def krope_half(half):
            tok = slice(t0 + half * 512, t0 + (half + 1) * 512)
            bx = fm_matmul(wkr2, 0, 64, half)
            bxs = fm_matmul(wkr2, 64, 64, half)
            rope_evac(bx, bxs, half, 1.0, KR_d[:, tok])

        def mk_fm(dst, idx0, tokoff, kindname):
            def fn(wb):
                for c in range(4):
                    for half in range(2):
                        tok = slice(tokoff + half * 512, tokoff + (half + 1) * 512)
                        bank = fm_matmul(wb, c * 128, 128, half)
                        if kindname == "silu":
                            stage_out(lambda e, o, bank=bank: e.activation(out=o, in_=bank.t[:], func=AF.Silu),
                                      "scalar", [bank], dst[idx0 + c, :, tok])
                        elif kindname == "fq":
                            stage_out(lambda e, o, bank=bank: e.activation(out=o, in_=bank.t[:], func=AF.Copy, scale=SCALE_FOX),
                                      "scalar", [bank], dst[idx0 + c, :, tok])
                        else:
                            stage_out(lambda e, o, bank=bank: e.tensor_copy(out=o, in_=bank.t[:]), "vector", [bank], dst[idx0 + c, :, tok])
            return fn

        def mk_fv(cg):
            def fn(wb):
                for blk in range(8):
                    bank = banks.next()
                    for kc in range(16):
                        op("tensor", lambda e, kc=kc, bank=bank, blk=blk: e.matmul(
                            bank.t[:], lhsT=hT.t[:, kc, blk * 128:(blk + 1) * 128], rhs=wb.t[:, kc, :],
                            start=(kc == 0), stop=(kc == 15)), reads=[hT, wb], writes=[bank])
                    if blk % 2 == 0:
                        op("vector", lambda e, bank=bank, blk=blk: e.tensor_copy(out=vstF.t[:, blk, :], in_=bank.t[:]),
                           reads=[bank], writes=[vstF])
                    else:
                        op("scalar", lambda e, bank=bank, blk=blk: e.activation(out=vstF.t[:, blk, :], in_=bank.t[:], func=AF.Copy),
                           reads=[bank], writes=[vstF])
                for h4 in range(4):
                    h = cg * 4 + h4
                    op("sync", lambda e, h=h, h4=h4: e.dma_start(out=VF_d[h, :, sg * 8:(sg + 1) * 8, :], in_=vstF.t[:, :, h4 * 128:(h4 + 1) * 128]),
                       reads=[vstF], dma=P.dma_sem("vstF"))
            return fn

        def t_flogit(wb):
            for blk in range(8):
                gb = sg * 8 + blk
                bank = banks.next()
                for kc in range(16):
                    op("tensor", lambda e, kc=kc, bank=bank, blk=blk: e.matmul(
                        bank.t[:, 0:8], lhsT=hT.t[:, kc, blk * 128:(blk + 1) * 128], rhs=wfl.t[:, kc, :],
                        start=(kc == 0), stop=(kc == 15)), reads=[hT, wfl], writes=[bank])
                op("vector", lambda e, bank=bank: e.tensor_tensor(out=zt.t[:], in0=bank.t[:, 0:8], in1=bf_bc.t[:], op=ALU.add),
                   reads=[bank, bf_bc], writes=[zt])
                op("vector", lambda e: e.tensor_scalar_mul(out=zt2.t[:], in0=zt.t[:], scalar1=-1.0), reads=[zt], writes=[zt2])
                op("vector", lambda e: e.tensor_tensor(out=zt2.t[:], in0=zt2.t[:], in1=zt.t[:], op=ALU.max), reads=[zt, zt2], writes=[zt2])
                op("scalar", lambda e: e.activation(out=zt2.t[:], in_=zt2.t[:], func=AF.Exp, scale=-1.0), reads=[zt2], writes=[zt2])
                op("vector", lambda e: e.tensor_scalar_add(out=zt2.t[:], in0=zt2.t[:], scalar1=1.0), reads=[zt2], writes=[zt2])
                op("scalar", lambda e: e.activation(out=zt2.t[:], in_=zt2.t[:], func=AF.Ln), reads=[zt2], writes=[zt2])
                op("vector", lambda e: e.tensor_scalar_min(out=zt3.t[:], in0=zt.t[:], scalar1=0.0), reads=[zt], writes=[zt3])
                op("vector", lambda e, gb=gb: e.tensor_tensor(out=LOGF.t[:, gb * 8:(gb + 1) * 8], in0=zt3.t[:], in1=zt2.t[:], op=ALU.subtract),
                   reads=[zt3, zt2], writes=[LOGF])

        if own:
            tasks.append((C_QLAT, 512, t_qlat0))
            tasks.append((C_QLAT + 512, 256, t_qlat1))
        tasks.append((C_KVLAT, 512, t_kvlat))
        tasks.append((None, 0, lambda wb: (krope_half(0), krope_half(1))))
        if own:
            tasks.append((C_GMLA, 512, mk_fm(GT_d, 0, q0, "silu")))
            tasks.append((C_GMLA + 512, 512, mk_fm(GT_d, 4, q0, "silu")))
            tasks.append((C_GFOX, 512, mk_fm(GT_d, 8, q0, "silu")))
            tasks.append((C_GFOX + 512, 512, mk_fm(GT_d, 12, q0, "silu")))
            tasks.append((C_FQ, 512, mk_fm(FQT_d, 0, q0, "fq")))
            tasks.append((C_FQ + 512, 512, mk_fm(FQT_d, 4, q0, "fq")))
        tasks.append((C_FK, 512, mk_fm(FKT_d, 0, t0, "copy")))
        tasks.append((C_FK + 512, 512, mk_fm(FKT_d, 4, t0, "copy")))
        tasks.append((C_FV, 512, mk_fv(0)))
        tasks.append((C_FV + 512, 512, mk_fv(1)))
        tasks.append((None, 0, t_flogit))
        loaded = {}
        wl = [i for i, t in enumerate(tasks) if t[0] is not None]
        for k, wb in enumerate(PRELOADED):
            assert (tasks[wl[k]][0], tasks[wl[k]][1]) == wb[0], (tasks[wl[k]][:2], wb[0])
            loaded[wl[k]] = wb[1]
        del PRELOADED[:]

        def ensure(upto):
            for i in wl:
                if i <= upto and i not in loaded:
                    loaded[i] = load_wgroup(tasks[i][0], tasks[i][1])
        for i, t in enumerate(tasks):
            nxt = [j for j in wl if j > i][:NWB - 1]
            ensure(max([i] + nxt))
            if sg == 0 and i == 0:
                load_resident_weights()
            t[2](loaded.get(i))
        nxt_sg = sg + 1 < nsg_total
        if nxt_sg:
            first = [(C_QLAT, 512), (C_QLAT + 512, 256)] if sg + 1 < 2 else [(C_KVLAT, 512), (C_FK, 512)]
            for (c0_, n_) in first[:NWB]:
                PRELOADED.append(((c0_, n_), load_wgroup(c0_, n_)))
            prepL(sg + 1, 0)
            prepL(sg + 1, 1)
            prepA(sg + 1, 0)
        for i in range(8):
            if nxt_sg and i + 1 < 8:
                prepA(sg + 1, i + 1)
            if nxt_sg and i + 2 < 8:
                prepL(sg + 1, i + 2)
            if own:
                qup_head(i)
            kup_head(i)
            vup_piece(i)
            if nxt_sg:
                prepB(sg + 1, i)

    nsg = 4
    if STOP_AFTER.startswith("A"):
        nsg = int(STOP_AFTER[1:])
    for sg in range(nsg):
        process_sg(sg, nsg)

    P.barrier()
    esA.close()
    esPA.close()
    esPB = contextlib.ExitStack()
    pbank = [psum_tile(esPB, "pbB%d" % i, [128, 512], F32) for i in range(8)]

    esB = contextlib.ExitStack()

    def sbB(name, shape, dt):
        t = T.__new__(T)
        t.t = esB.enter_context(nc.sbuf_tensor("sc_" + name, shape, dt))
        t.b = Buf(name)
        return t

    if not STOP_AFTER.startswith("A"):
        mpred = sbB("mpred", [32, 32], F32)
        Tt = sbB("Tt", [32, 8], F32)
        Xp = sbB("Xp", [32, 256], F32)
        op("sync", lambda e: e.dma_start(out=mpred.t[:], in_=mpred_d), writes=[mpred], dma=dsem(mpred))
        bW, bT, bP = pbank[0], pbank[1], pbank[2]
        op("tensor", lambda e: e.matmul(bW.t[:, 0:256], lhsT=utri_f.t[:], rhs=LOGF.t[:], start=True, stop=True),
           reads=[utri_f, LOGF], writes=[bW])
        for h in range(8):
            op("tensor", lambda e, h=h: e.matmul(bT.t[0:32, h:h + 1], lhsT=LOGF.t[:, h:256:8], rhs=ones_f.t[:, 0:1], start=True, stop=True),
               reads=[LOGF, ones_f], writes=[bT])
        op("vector", lambda e: e.tensor_copy(out=Tt.t[:], in_=bT.t[0:32, 0:8]), reads=[bT], writes=[Tt])
        for h in range(8):
            op("vector", lambda e, h=h: e.tensor_scalar(out=Xp.t[:, h:256:8], in0=mpred.t[:], scalar1=Tt.t[:, h:h + 1], scalar2=None, op0=ALU.mult),
               reads=[mpred, Tt], writes=[Xp])
        op("tensor", lambda e: e.matmul(bP.t[:, 0:256], lhsT=ones_f.t[0:32, :], rhs=Xp.t[:], start=True, stop=True),
           reads=[ones_f, Xp], writes=[bP])
        op("vector", lambda e: e.tensor_copy(out=CC.t[:], in_=bW.t[:, 0:256]), reads=[bW], writes=[CC])
        op("vector", lambda e: e.tensor_tensor(out=CC.t[:], in0=CC.t[:], in1=bP.t[:, 0:256], op=ALU.add), reads=[CC, bP], writes=[CC])
        op("vector", lambda e: e.tensor_scalar_mul(out=NEGC.t[:], in0=CC.t[:], scalar1=-1.0), reads=[CC], writes=[NEGC])
        cct = sbB("cct", [128, 128], F32)
        bC = pbank[3]
        op("tensor", lambda e: e.matmul(bC.t[:, 0:128], lhsT=CC.t[:, 0:128], rhs=ident_f.t[:], start=True, stop=True),
           reads=[CC, ident_f], writes=[bC])
        op("vector", lambda e: e.tensor_copy(out=cct.t[:], in_=bC.t[:, 0:128]), reads=[bC], writes=[cct])
        cctv = CCT_d.rearrange("(h s) t -> s h t", s=16)
        for sl in range(16):
            op("sync", lambda e, sl=sl: e.dma_start(out=cctv[sl], in_=cct.t[sl * 8:(sl + 1) * 8, :]), reads=[cct], dma=P.dma_sem("cctst"))
        P.barrier()
        for ks in range(16):
            op("vector", lambda e, ks=ks: e.tensor_scalar(out=NEGCF.t[:, ks * 8:(ks + 1) * 8], in0=NEGC.t[:, (16 + ks) * 8:(17 + ks) * 8],
                                                          scalar1=flag_bc.t[:, ks % 2:ks % 2 + 1], scalar2=None, op0=ALU.add),
               reads=[NEGC, flag_bc], writes=[NEGCF])
        if DEBUG:
            op("sync", lambda e: e.dma_start(out=LOGF_d, in_=LOGF.t[:]), reads=[LOGF], dma=ld_misc)
            op("sync", lambda e: e.dma_start(out=CC_d, in_=CC.t[:]), reads=[CC], dma=ld_misc)

    if not STOP_AFTER:
        wout = sbB("wout", [128, 16, D], BF16)
        gpost_bc = sbB("gpost_bc", [128, D], F32)
        op("sync", lambda e: e.dma_start(out=gpost_bc.t[:], in_=gpost_d.partition_broadcast(128)), writes=[gpost_bc], dma=dsem(gpost_bc))
        OG = [sbB("OG%d" % i, [128, 16, 512], BF16) for i in range(2)]
        kt = [sbB("kt%d" % i, [128, 2, 2048], BF16) for i in range(2)]
        vt = [sbB("vt%d" % i, [128, 2, 16, 128], BF16) for i in range(2)]
        qt = [sbB("qt%d" % i, [128, 512], BF16) for i in range(2)]
        qr = [sbB("qr%d" % i, [128, 512], BF16) for i in range(2)]
        gt = [sbB("gt%d" % i, [128, 512], BF16) for i in range(2)]
        kr = [sbB("kr%d" % i, [128, 2, 2048], BF16) for i in range(1)]
        op("gpsimd", lambda e: e.memset(kr[0].t[64:128, :, :], 0.0), writes=[kr[0]])
        for i in range(2):
            op("gpsimd", lambda e, i=i: e.memset(qr[i].t[64:128, :], 0.0), writes=[qr[i]])
        cqtri = [sbB("cqtri%d" % i, [128, 512], F32) for i in range(2)]
        dacc = [sbB("dacc%d" % i, [128, 512], F32) for i in range(2)]
        ld_sem = [P.dma_sem("ldB%d" % i) for i in range(2)]
        kr_sem = [P.dma_sem("ldkr%d" % i) for i in range(1)]
        cqrow = [sbB("cqrow%d" % i, [128, 512], F32) for i in range(2)]
        dgt = Rot([sbB("dg%d" % i, [128, 128], F32) for i in range(2)])
        pt = Rot([sbB("pt%d" % i, [128, 512], BF16) for i in range(4)])
        tmpF = Rot([sbB("tmpF%d" % i, [128, 512], F32) for i in range(3)])
        tmpS = Rot([sbB("tmpS%d" % i, [128, 128], F32) for i in range(2)])
        rden = sbB("rden", [128, 512], F32)
        ysb = sbB("ysb", [128, D], F32)
        ysq = sbB("ysq", [128, 512], BF16)
        xres = [sbB("xres%d" % i, [128, D], F32) for i in range(1)]
        xres_sem = [P.dma_sem("ldxr%d" % i) for i in range(1)]
        out_sem = [P.dma_sem("stout%d" % i) for i in range(1)]
        ssy = sbB("ssy", [128, 4], F32)
        rsy = sbB("rsy", [128, 1], F32)

        for i in range(4):
            op("gpsimd", lambda e, i=i: e.dma_start(out=wout.t[:, i * 4:(i + 1) * 4, :],
                                                    in_=w_out_d.rearrange("(kc p) n -> p kc n", p=128)[:, i * 4:(i + 1) * 4, :]),
               writes=[wout], dma=P.dma_sem("ldwout"))
        Sb = Rot(pbank[0:4])
        ob = [pbank[4], pbank[5]]
        db = [pbank[6], pbank[7]]

        def make_head(g, hh, bi, krb, ogb):
            nk = 4 * g + 4
            mla = hh < 8
            h = hh % 8
            KTs = KT_d if mla else FKT_d
            Vs = VM_d if mla else VF_d
            Qs = QT_d if mla else FQT_d
            ktb, vtb, qtb, qrb, gtb = kt[bi], vt[bi], qt[bi], qr[bi], gt[bi]
            cqb = cqrow[bi]
            cqt = cqtri[bi]
            dac = dacc[bi]
            def loads():
                for kind in range(2):
                    op("sync", lambda e, kind=kind: e.dma_start(out=ktb.t[:, kind, 0:nk * 128], in_=KTs[h, :, kind * 2048:kind * 2048 + nk * 128]),
                       writes=[ktb], dma=dsem(ktb))
                    op("sync", lambda e, kind=kind: e.dma_start(out=vtb.t[:, kind, 0:nk, :], in_=Vs[h, :, kind * 16:kind * 16 + nk, :]),
                       writes=[vtb], dma=dsem(vtb))
                op("sync", lambda e: e.dma_start(out=qtb.t[:], in_=Qs[h, 0:128, g * 512:(g + 1) * 512]), writes=[qtb], dma=dsem(qtb))
                if mla:
                    op("sync", lambda e: e.dma_start(out=qrb.t[0:64, :], in_=QT_d[h, 128:192, g * 512:(g + 1) * 512]), writes=[qrb], dma=dsem(qrb))
                if not mla and not OLDCQ:
                    r0 = h * 16 + 4 * g
                    op("sync", lambda e: e.dma_start(out=cqb.t[:], in_=CCT_d[r0:r0 + 4, :].rearrange("(o r) t -> o (r t)", o=1).partition_broadcast(128)),
                       writes=[cqb], dma=dsem(cqb))
            def old_cq():
                for r in range(4):
                    col = (4 * g + r) * 8 + h
                    dg = dgt.next()
                    op("vector", lambda e, dg=dg, col=col: e.tensor_scalar(out=dg.t[:], in0=ident_f.t[:], scalar1=CC.t[:, col:col + 1], scalar2=None,
                                                                           op0=ALU.mult), reads=[ident_f, CC], writes=[dg])
                    bank = Sb.next()
                    op("tensor", lambda e, dg=dg, bank=bank: e.matmul(bank.t[:, 0:128], lhsT=ones_f.t[:], rhs=dg.t[:], start=True, stop=True),
                       reads=[ones_f, dg], writes=[bank])
                    op("vector", lambda e, bank=bank, r=r: e.tensor_copy(out=cqb.t[:, r * 128:(r + 1) * 128], in_=bank.t[:, 0:128]),
                       reads=[bank], writes=[cqb])
                    op("vector", lambda e, bank=bank, r=r: e.tensor_tensor(out=cqt.t[:, r * 128:(r + 1) * 128], in0=bank.t[:, 0:128], in1=trimask.t[:], op=ALU.add),
                       reads=[bank, trimask], writes=[cqt])
            def load_gate():
                op("gpsimd", lambda e: e.dma_start(out=gtb.t[:], in_=GT_d[hh, :, g * 512:(g + 1) * 512]), writes=[gtb], dma=dsem(gtb))

            def setup():
                if not mla and OLDCQ:
                    old_cq()
                elif not mla:
                    for r in range(4):
                        op("vector", lambda e, r=r: e.tensor_tensor(out=cqt.t[:, r * 128:(r + 1) * 128], in0=cqb.t[:, r * 128:(r + 1) * 128],
                                                                    in1=trimask.t[:], op=ALU.add), reads=[cqb, trimask], writes=[cqt])

            fulls, specs = [], []
            for kind in range(2):
                for ks in range(nk):
                    r = ks - 4 * g
                    c0 = 0 if r < 0 else r * 128
                    (specs if r >= 0 else fulls).append((kind, ks, c0, r >= 0))
            if fulls:
                tiles = [fulls.pop(0)]
                step = max(1, len(fulls) // len(specs)) if specs else 1
                fi = 0
                for sp in specs:
                    tiles.extend(fulls[fi:fi + step])
                    fi += step
                    tiles.append(sp)
                tiles.extend(fulls[fi:])
            else:
                tiles = specs
            obank, dbank = ob[bi], db[bi]
            nt = len(tiles)

            def qk(ti):
                kind, ks, c0, special = tiles[ti]
                bank = Sb.next()
                if mla:
                    op("tensor", lambda e: e.matmul(bank.t[:, c0:512], lhsT=ktb.t[:, kind, ks * 128:(ks + 1) * 128], rhs=qtb.t[:, c0:512],
                                                    start=True, stop=False), reads=[ktb, qtb], writes=[bank])
                    op("tensor", lambda e: e.matmul(bank.t[:, c0:512], lhsT=krb.t[:, kind, ks * 128:(ks + 1) * 128], rhs=qrb.t[:, c0:512],
                                                    start=False, stop=True), reads=[krb, qrb], writes=[bank])
                else:
                    op("tensor", lambda e: e.matmul(bank.t[:, c0:512], lhsT=ktb.t[:, kind, ks * 128:(ks + 1) * 128], rhs=qtb.t[:, c0:512],
                                                    start=True, stop=True), reads=[ktb, qtb], writes=[bank])
                return bank

            def softmax_part(ti, bank):
                kind, ks, c0, special = tiles[ti]
                p = pt.next()
                sl = ks
                c1 = c0 + 128
                if mla:
                    op("scalar", lambda e: e.activation(out=p.t[:, c0:512], in_=bank.t[:, c0:512], func=AF.Exp), reads=[bank], writes=[p])
                    if special:
                        if kind == 0:
                            op("vector", lambda e: e.tensor_tensor(out=p.t[:, c0:c1], in0=p.t[:, c0:c1], in1=tri01.t[:], op=ALU.mult),
                               reads=[p, tri01], writes=[p])
                        else:
                            op("vector", lambda e: e.tensor_scalar(out=p.t[:, c0:c1], in0=p.t[:, c0:c1], scalar1=flag01.t[:, sl % 2:sl % 2 + 1],
                                                                   scalar2=None, op0=ALU.mult), reads=[p, flag01], writes=[p])
                    if ti == 0:
                        op("vector", lambda e: e.tensor_copy(out=dac.t[:, c0:512], in_=p.t[:, c0:512]), reads=[p], writes=[dac])
                    else:
                        op("vector", lambda e: e.tensor_tensor(out=dac.t[:, c0:512], in0=dac.t[:, c0:512], in1=p.t[:, c0:512], op=ALU.add),
                           reads=[p, dac], writes=[dac])
                else:
                    tf = tmpF.next()
                    col = (kind * 16 + ks) * 8 + h
                    nc_ = NEGC.t[:, col:col + 1]

                    def add(lo, hi, cq, sc):
                        op("vector", lambda e: e.scalar_tensor_tensor(out=tf.t[:, lo:hi], in0=bank.t[:, lo:hi], scalar=sc, in1=cq.t[:, lo:hi],
                                                                      op0=ALU.add, op1=ALU.add), reads=[bank, cq, NEGC, NEGCF], writes=[tf])
                    if special and kind == 0:
                        add(c0, c1, cqt, nc_)
                        if c1 < 512:
                            add(c1, 512, cqb, nc_)
                    elif special:
                        colf = ks * 8 + h
                        add(c0, c1, cqb, NEGCF.t[:, colf:colf + 1])
                        if c1 < 512:
                            add(c1, 512, cqb, nc_)
                    else:
                        add(c0, 512, cqb, nc_)
                    op("scalar", lambda e: e.activation(out=p.t[:, c0:512], in_=tf.t[:, c0:512], func=AF.Exp), reads=[tf], writes=[p])
                return p

            def pv(ti, p):
                kind, ks, c0, special = tiles[ti]
                op("tensor", lambda e: e.matmul(obank.t[:, c0:512], lhsT=vtb.t[:, kind, ks, :], rhs=p.t[:, c0:512],
                                                start=(ti == 0), stop=(ti == nt - 1)), reads=[vtb, p], writes=[obank])
                if not mla:
                    op("tensor", lambda e: e.matmul(dbank.t[:, c0:512], lhsT=ones_bf.t[:], rhs=p.t[:, c0:512],
                                                    start=(ti == 0), stop=(ti == nt - 1)), reads=[ones_bf, p], writes=[dbank])

            def finish_pe():
                if mla:
                    op("tensor", lambda e: e.matmul(dbank.t[:], lhsT=ones_f.t[:], rhs=dac.t[:], start=True, stop=True),
                       reads=[ones_f, dac], writes=[dbank])

            def finish_act():
                op("scalar", lambda e: e.activation(out=rden.t[:], in_=dbank.t[:], func=AF.Ln), reads=[dbank], writes=[rden])
                op("scalar", lambda e: e.activation(out=rden.t[:], in_=rden.t[:], func=AF.Exp, scale=-1.0), reads=[rden], writes=[rden])

            def finish():
                op("vector", lambda e: e.tensor_tensor(out=rden.t[:], in0=obank.t[:], in1=rden.t[:], op=ALU.mult),
                   reads=[obank, rden], writes=[rden])
                op("gpsimd", lambda e: e.tensor_tensor(out=ogb.t[:, hh, :], in0=rden.t[:], in1=gtb.t[:], op=ALU.mult),
                   reads=[rden, gtb], writes=[ogb])

            class H:
                pass
            H.loads, H.setup, H.qk, H.softmax_part, H.pv, H.finish, H.nt, H.load_gate, H.finish_pe, H.finish_act = loads, setup, qk, softmax_part, pv, finish, nt, load_gate, finish_pe, finish_act
            return H

        def load_kr(g):
            nk = 4 * g + 4
            for kind in range(2):
                op("sync", lambda e, kind=kind: e.dma_start(out=kr[0].t[0:64, kind, 0:nk * 128], in_=KR_d[:, kind * 2048:kind * 2048 + nk * 128]),
                   writes=[kr[0]], dma=kr_sem[0])

        NGROUPS = int(os.environ.get("MK_NGROUPS", "4"))
        ALLH = [[make_head(g, hh, (g * 16 + hh) % 2, kr[0], OG[g % 2]) for hh in range(16)] for g in range(NGROUPS)]

        def do_group(g, prevC):
            nk = 4 * g + 4
            ogb = OG[g % 2]
            heads = ALLH[g]
            nxt_heads = ALLH[g + 1] if g + 1 < NGROUPS else None
            jobs = [(hi, ti) for hi in range(16) for ti in range(heads[hi].nt)]
            LOOK = 3
            pend = []
            state = {"i": 0}

            def issue():
                hi, ti = jobs[state["i"]]
                state["i"] += 1
                H = heads[hi]
                if ti == 0:
                    H.setup()
                pend.append((hi, ti, H.qk(ti)))
            if g == 0:
                load_kr(0)
                heads[0].loads()
                heads[0].load_gate()
                heads[1].loads()
                heads[1].load_gate()
            DEFER = 2
            finq = []

            def defer(n, fn):
                finq.append([n, fn])

            def run_finq(force=False):
                for item in finq:
                    item[0] -= 1
                while finq and (force or finq[0][0] <= 0):
                    finq.pop(0)[1]()
            for _ in range(LOOK):
                issue()
            stride = max(1, len(jobs) // (len(prevC) + 2)) if prevC else 0
            cnt = 0
            while pend:
                hi, ti, bank = pend.pop(0)
                H = heads[hi]
                p = H.softmax_part(ti, bank)
                if state["i"] < len(jobs):
                    issue()
                H.pv(ti, p)
                cnt += 1
                if prevC and cnt % stride == 0:
                    prevC.pop(0)()
                run_finq()
                if ti == H.nt - 4 and hi >= 1:
                    heads[hi - 1].finish_act()
                if ti == H.nt - 1:
                    if hi + 2 < 16:
                        heads[hi + 2].loads()
                    elif nxt_heads is not None:
                        nxt_heads[hi + 2 - 16].loads()
                    if hi == 7 and nxt_heads is not None:
                        load_kr(g + 1)
                    defer(DEFER, H.finish_pe)
                    if hi >= 1:
                        heads[hi - 1].finish()
                        if hi + 1 < 16:
                            heads[hi + 1].load_gate()
                        elif nxt_heads is not None:
                            nxt_heads[0].load_gate()
            run_finq(force=True)
            heads[15].finish_act()
            heads[15].finish()
            if nxt_heads is not None:
                nxt_heads[1].load_gate()
            units = []
            for tb in range(4):
                row0 = (g * 4 + tb) * 128

                def u_start(row0=row0):
                    op("gpsimd", lambda e: e.dma_start(out=xres[0].t[:], in_=x_d[row0:row0 + 128, :]),
                       writes=[xres[0]], dma=xres_sem[0])
                    op("gpsimd", lambda e: e.memset(ssy.t[:], 0.0), writes=[ssy])

                def u_cp(cp, tb=tb, first=False, row0=row0):
                    if first:
                        u_start(row0)
                    bank = Sb.next()
                    for hh in range(16):
                        op("tensor", lambda e, hh=hh: e.matmul(
                            bank.t[:], lhsT=ogb.t[:, hh, tb * 128:(tb + 1) * 128], rhs=wout.t[:, hh, cp * 512:(cp + 1) * 512],
                            start=(hh == 0), stop=(hh == 15)), reads=[ogb, wout], writes=[bank])
                    op("scalar", lambda e: e.activation(out=ysb.t[:, cp * 512:(cp + 1) * 512], in_=bank.t[:], func=AF.Copy),
                       reads=[bank], writes=[ysb])
                    op("scalar", lambda e: e.activation(out=ysq.t[:], in_=bank.t[:], func=AF.Square, accum_out=ssy.t[:, cp:cp + 1]),
                       reads=[bank], writes=[ysq, ssy])

                def u_end(row0=row0):
                    op("vector", lambda e: e.tensor_reduce(out=rsy.t[:], in_=ssy.t[:], axis=mybir.AxisListType.X, op=ALU.add), reads=[ssy], writes=[rsy])
                    op("vector", lambda e: e.tensor_scalar(out=rsy.t[:], in0=rsy.t[:], scalar1=1.0 / D, scalar2=EPS, op0=ALU.mult, op1=ALU.add),
                       reads=[rsy], writes=[rsy])
                    op("scalar", lambda e: e.activation(out=rsy.t[:], in_=rsy.t[:], func=AF.Ln), reads=[rsy], writes=[rsy])
                    op("scalar", lambda e: e.activation(out=rsy.t[:], in_=rsy.t[:], func=AF.Exp, scale=-0.5), reads=[rsy], writes=[rsy])
                    xr = xres[0]
                    op("vector", lambda e: e.scalar_tensor_tensor(out=ysb.t[:], in0=ysb.t[:], scalar=rsy.t[:, 0:1], in1=gpost_bc.t[:],
                                                                  op0=ALU.mult, op1=ALU.mult), reads=[ysb, rsy, gpost_bc], writes=[ysb])
                    op("gpsimd", lambda e: e.tensor_tensor(out=xr.t[:], in0=ysb.t[:], in1=xr.t[:], op=ALU.add),
                       reads=[ysb, xr], writes=[xr])
                    op("gpsimd", lambda e: e.dma_start(out=out_d[row0:row0 + 128, :], in_=xr.t[:]),
                       reads=[xr], dma=out_sem[0])
                for cp in range(4):
                    units.append(lambda cp=cp, u_cp=u_cp: u_cp(cp, first=(cp == 0)))
                units.append(u_end)
            return units

        prevC = []
        for g in range(NGROUPS):
            nxt = do_group(g, prevC)
            while prevC:
                prevC.pop(0)()
            prevC = nxt
        while prevC:
            prevC.pop(0)()

    P.emit()
    esB.close()
    esPB.close()
    return nc


def _perm(half):
    own = [b for b in range(NBLK) if ((b % 4) in (0, 3)) == (half == 0)]
    oth = [b for b in range(NBLK) if b not in own]
    return own, oth


def _host_inputs(core, x, positions, g_pre, w_in, g_q_latent, w_uq, g_kv_latent, w_ukv, b_forget, w_out, g_post, shared):
    b, half = core // 2, core % 2
    own, oth = _perm(half)
    perm = np.array(own + oth)
    xb = np.ascontiguousarray(x[b].reshape(NBLK, 128, D)[perm].reshape(SEQ, D))
    pb = np.ascontiguousarray(positions[b].reshape(NBLK, 128)[perm].reshape(1, SEQ)).astype(np.int32)
    mpred = (perm[:, None] < perm[None, :]).astype(np.float32)
    if half == 0:
        flag = np.array([[NEG, 0.0]], np.float32)
    else:
        flag = np.array([[0.0, NEG]], np.float32)
    m = dict(shared)
    m.update({"x": xb, "pos": pb, "mpred": np.ascontiguousarray(mpred), "flag": flag})
    return m


def _shared_inputs(g_pre, w_in, g_q_latent, w_uq, g_kv_latent, w_ukv, b_forget, w_out, g_post):
    f32 = np.float32
    w_in0 = np.ascontiguousarray(w_in[0], dtype=f32)
    w_uq0 = np.ascontiguousarray(w_uq[0], dtype=f32)
    w_ukv0 = np.asarray(w_ukv[0], dtype=f32)
    sw = np.concatenate([np.arange(32, 64), np.arange(0, 32)])
    uq3 = w_uq0.reshape(768, 8, 192)
    w_uq_sw = np.ascontiguousarray(uq3[:, :, 128:][:, :, sw].reshape(768, 512))
    kr = w_in0[:, C_KROPE:C_KROPE + 64]
    w_kr2 = np.ascontiguousarray(np.concatenate([kr, kr[:, sw]], axis=1))
    ukv4 = w_ukv0.reshape(512, 8, 2, 128)
    w_uk = np.ascontiguousarray(ukv4[:, :, 0, :].reshape(512, 1024))
    w_uv = np.ascontiguousarray(ukv4[:, :, 1, :].reshape(512, 1024))
    inv_freq = (10000.0 ** (-np.arange(0, 64, 2, dtype=np.float64) / 64.0))
    cst = np.zeros((64, 4), np.float64)
    cst[:, 0] = np.concatenate([inv_freq, inv_freq]) / TWO_PI
    cst[:32, 1] = 0.5
    cst[32:, 1] = 0.0
    cst[:, 2] = 0.25
    return {
        "g_pre": np.ascontiguousarray(g_pre[0:1], dtype=f32),
        "g_post": np.ascontiguousarray(g_post[0:1], dtype=f32),
        "gq_col": np.ascontiguousarray(np.asarray(g_q_latent[0], f32).reshape(6, 128).T),
        "gkv_col": np.ascontiguousarray(np.asarray(g_kv_latent[0], f32).reshape(4, 128).T),
        "b_forget": np.ascontiguousarray(b_forget[0:1], dtype=f32),
        "cst": cst.astype(f32),
        "w_in": w_in0, "w_uq": w_uq0, "w_uq_sw": w_uq_sw, "w_uk": w_uk, "w_uv": w_uv, "w_kr2": w_kr2,
        "w_out": np.ascontiguousarray(w_out[0], dtype=f32),
    }


_NC_CACHE = {}


def kernel(x, positions, g_pre, w_in, g_q_latent, w_uq, g_kv_latent, w_ukv, b_forget, w_out, g_post, _cores=None, _raw=False):
    x = np.asarray(x)
    positions = np.asarray(positions)
    shared = _shared_inputs(np.asarray(g_pre), np.asarray(w_in), np.asarray(g_q_latent), np.asarray(w_uq),
                            np.asarray(g_kv_latent), np.asarray(w_ukv), np.asarray(b_forget), np.asarray(w_out), np.asarray(g_post))
    cores = list(range(8)) if _cores is None else _cores
    in_maps = [_host_inputs(c, x, positions, None, None, None, None, None, None, None, None, None, shared) for c in cores]
    if "nc" not in _NC_CACHE:
        _NC_CACHE["nc"] = build_program()
    nc = _NC_CACHE["nc"]
    res = run_bass_kernel_spmd(nc, in_maps, core_ids=list(range(len(cores))))
    if _raw:
        return res
    out = np.zeros((4, SEQ, D), np.float32)
    for i, c in enumerate(cores):
        b, half = c // 2, c % 2
        own, _ = _perm(half)
        o = np.asarray(res.results[i]["out"]).reshape(16, 128, D)
        ob = out[b].reshape(NBLK, 128, D)
        for s, blk in enumerate(own):
            ob[blk] = o[s]
    return out
```

```python
import os
import contextlib
import numpy as np
import concourse.bass as bass
import concourse.mybir as mybir
from concourse.bass_utils import run_bass_kernel_spmd

F32 = mybir.dt.float32
BF16 = mybir.dt.bfloat16
I32 = mybir.dt.int32
AF = mybir.ActivationFunctionType
ALU = mybir.AluOpType

D = 2048
SEQ = 4096
NBLK = 32
D_IN = 6472
EPS = 1e-6
NEG = -30000.0
SCALE_MLA = 192 ** -0.5
SCALE_FOX = 128 ** -0.5
TWO_PI = 2.0 * np.pi
SIN_SCALE = TWO_PI * (1.0 - 1e-6)
C_QLAT, C_KVLAT, C_KROPE, C_GMLA, C_FQ, C_FK, C_FV, C_FLOG, C_GFOX = 0, 768, 1280, 1344, 2368, 3392, 4416, 5440, 5448

DEBUG = bool(int(os.environ.get("MK_DEBUG", "0")))
STOP_AFTER = os.environ.get("MK_STOP", "")


class Buf:
    __slots__ = ("w", "r", "name")

    def __init__(self, name=""):
        self.w = None
        self.r = {}
        self.name = name


class T:
    def __init__(self, nc, name, shape, dt, psum=False):
        if psum:
            self.t = nc.alloc_psum_tensor("ps_" + name, shape, dt)
        else:
            self.t = nc.alloc_sbuf_tensor("sb_" + name, shape, dt)
        self.b = Buf(name)


def _b(x):
    return x.b if isinstance(x, T) else x


class EngQ:
    def __init__(self, name):
        self.name = name
        self.ops = []
        self.waited = {}


class Prog:
    ENGS = ["tensor", "vector", "scalar", "gpsimd", "sync"]
    SEM_LIMIT = 28000

    def __init__(self, nc):
        self.nc = nc
        self.q = {n: EngQ(n) for n in self.ENGS}
        self.semcount = {}
        self.semkeys = []
        self.cur = {}
        self.gen = {}
        for n in self.ENGS:
            self.gen[n] = 0
            self.cur[n] = "E_%s_0" % n
            self._mksem(self.cur[n])

    def _mksem(self, key):
        if key not in self.semcount:
            self.semcount[key] = 0
            self.semkeys.append(key)

    def dma_sem(self, key):
        key = "D_" + key
        self._mksem(key)
        return key

    def op(self, eng, fn, reads=(), writes=(), dma=None):
        q = self.q[eng]
        need = {}
        for b in reads:
            b = _b(b)
            if b.w is not None:
                k, v = b.w
                if need.get(k, 0) < v:
                    need[k] = v
        for b in writes:
            b = _b(b)
            if b.w is not None:
                k, v = b.w
                if need.get(k, 0) < v:
                    need[k] = v
            for k, v in b.r.items():
                if need.get(k, 0) < v:
                    need[k] = v
        own = self.cur[eng]
        for k, v in need.items():
            if eng == "tensor" and dma is None and k.startswith("E_tensor"):
                continue
            if q.waited.get(k, 0) < v:
                q.waited[k] = v
                q.ops.append(("wait", k, v))
        if dma is not None:
            if dma.startswith("D_") is False:
                dma = self.dma_sem(dma)
            if self.semcount[dma] + 16 > self.SEM_LIMIT:
                raise RuntimeError("dma sem overflow " + dma)
            self.semcount[dma] += 16
            ev = (dma, self.semcount[dma])
            q.ops.append(("op", fn, dma, 16))
        else:
            if self.semcount[own] + 1 > self.SEM_LIMIT:
                self.gen[eng] += 1
                own = "E_%s_%d" % (eng, self.gen[eng])
                self.cur[eng] = own
                self._mksem(own)
            self.semcount[own] += 1
            ev = (own, self.semcount[own])
            q.ops.append(("op", fn, own, 1))
        for b in writes:
            b = _b(b)
            b.w = ev
            b.r = {}
        for b in reads:
            b = _b(b)
            if b.r.get(ev[0], 0) < ev[1]:
                b.r[ev[0]] = ev[1]
        return ev

    def barrier(self):
        for n in self.ENGS:
            q = self.q[n]
            for k in self.semkeys:
                v = self.semcount[k]
                if v > 0 and q.waited.get(k, 0) < v:
                    q.waited[k] = v
                    q.ops.append(("wait", k, v))

    def emit(self):
        nc = self.nc
        self.barrier()
        with contextlib.ExitStack() as es:
            sems = {}
            for k in self.semkeys:
                sems[k] = es.enter_context(nc.semaphore(k))
            block = es.enter_context(nc.Block())

            def make(engname):
                def body(e):
                    for item in self.q[engname].ops:
                        if item[0] == "wait":
                            e.wait_ge(sems[item[1]], item[2])
                        else:
                            _, fn, k, inc = item
                            fn(e).then_inc(sems[k], inc)
                return body
            block.tensor(make("tensor"))
            block.vector(make("vector"))
            block.scalar(make("scalar"))
            block.gpsimd(make("gpsimd"))
            block.sync(make("sync"))


class Rot:
    def __init__(self, tiles):
        self.tiles = tiles
        self.i = 0

    def next(self):
        t = self.tiles[self.i % len(self.tiles)]
        self.i += 1
        return t


def build_program():
    nc = bass.Bass("TRN2", target_bir_lowering=False)
    P = Prog(nc)
    op = P.op

    def dsem(t):
        return P.dma_sem('w_' + t.b.name)

    def din(name, shape, dt=F32):
        return nc.dram_tensor(name, shape, dt, kind="ExternalInput").ap()

    def dscr(name, shape, dt=BF16):
        if DEBUG:
            return nc.dram_tensor(name, shape, dt, kind="ExternalOutput").ap()
        return nc.dram_tensor(name, shape, dt).ap()

    x_d = din("x", [SEQ, D])
    pos_d = din("pos", [1, SEQ], I32)
    gpre_d = din("g_pre", [1, D])
    gpost_d = din("g_post", [1, D])
    gq_d = din("gq_col", [128, 6])
    gkv_d = din("gkv_col", [128, 4])
    bf_d = din("b_forget", [1, 8])
    cst_d = din("cst", [64, 4])
    flag_d = din("flag", [1, 2])
    mpred_d = din("mpred", [32, 32])
    w_in_d = din("w_in", [D, D_IN])
    w_uq_d = din("w_uq", [768, 1536])
    w_uqsw_d = din("w_uq_sw", [768, 512])
    w_uk_d = din("w_uk", [512, 1024])
    w_uv_d = din("w_uv", [512, 1024])
    w_kr2_d = din("w_kr2", [D, 128])
    w_out_d = din("w_out", [D, D])
    out_d = nc.dram_tensor("out", [2048, D], F32, kind="ExternalOutput").ap()

    QT_d = dscr("QT", [8, 192, 2048])
    KT_d = dscr("KT", [8, 128, SEQ])
    KR_d = dscr("KR", [64, SEQ])
    VM_d = dscr("VM", [8, 128, NBLK, 128])
    FQT_d = dscr("FQT", [8, 128, 2048])
    FKT_d = dscr("FKT", [8, 128, SEQ])
    VF_d = dscr("VF", [8, 128, NBLK, 128])
    GT_d = dscr("GT", [16, 128, 2048])
    CCT_d = nc.dram_tensor("CCT", [128, 128], F32).ap()
    if DEBUG:
        LOGF_d = dscr("LOGF_dbg", [128, 256], F32)
        CC_d = dscr("CC_dbg", [128, 256], F32)

    w_in_v = w_in_d.rearrange("(kc p) n -> p kc n", p=128)

    ident_f = T(nc, "ident_f", [128, 128], F32)
    ident_bf = T(nc, "ident_bf", [128, 128], BF16)
    ones_bf = T(nc, "ones_bf", [128, 128], BF16)
    ones_f = T(nc, "ones_f", [128, 128], F32)
    utri_f = T(nc, "utri_f", [128, 128], F32)
    trimask = T(nc, "trimask", [128, 128], F32)
    LOGF = T(nc, "LOGF", [128, 256], F32)
    CC = T(nc, "CC", [128, 256], F32)
    NEGC = T(nc, "NEGC", [128, 256], F32)
    NEGCF = T(nc, "NEGCF", [128, 128], F32)
    flag_bc = T(nc, "flag_bc", [128, 2], F32)
    flag01 = T(nc, "flag01", [128, 2], F32)
    tri01 = T(nc, "tri01", [128, 128], BF16)

    ld_misc = P.dma_sem("ld_misc")
    op("gpsimd", lambda e: e.memset(ident_f.t[:], 0.0), writes=[ident_f])
    op("gpsimd", lambda e: e.affine_select(out=ident_f.t[:], in_=ident_f.t[:], pattern=[[-1, 128]],
                                           compare_op=ALU.not_equal, fill=1.0, base=0, channel_multiplier=1),
       reads=[ident_f], writes=[ident_f])
    op("vector", lambda e: e.tensor_copy(out=ident_bf.t[:], in_=ident_f.t[:]), reads=[ident_f], writes=[ident_bf])
    op("gpsimd", lambda e: e.memset(ones_bf.t[:], 1.0), writes=[ones_bf])
    op("gpsimd", lambda e: e.memset(ones_f.t[:], 1.0), writes=[ones_f])
    op("gpsimd", lambda e: e.memset(utri_f.t[:], 1.0), writes=[utri_f])
    op("gpsimd", lambda e: e.affine_select(out=utri_f.t[:], in_=utri_f.t[:], pattern=[[1, 128]],
                                           compare_op=ALU.is_ge, fill=0.0, base=0, channel_multiplier=-1),
       reads=[utri_f], writes=[utri_f])
    op("gpsimd", lambda e: e.memset(trimask.t[:], 0.0), writes=[trimask])
    op("gpsimd", lambda e: e.affine_select(out=trimask.t[:], in_=trimask.t[:], pattern=[[1, 128]],
                                           compare_op=ALU.is_ge, fill=NEG, base=0, channel_multiplier=-1),
       reads=[trimask], writes=[trimask])
    op("sync", lambda e: e.dma_start(out=flag_bc.t[:], in_=flag_d.partition_broadcast(128)), writes=[flag_bc], dma=dsem(flag_bc))
    op("vector", lambda e: e.tensor_scalar(out=flag01.t[:], in0=flag_bc.t[:], scalar1=1.0 / 30000.0, scalar2=1.0, op0=ALU.mult, op1=ALU.add),
       reads=[flag_bc], writes=[flag01])
    op("vector", lambda e: e.tensor_copy(out=tri01.t[:], in_=utri_f.t[:]), reads=[utri_f], writes=[tri01])

    esPA = contextlib.ExitStack()

    def psum_tile(es, name, shape, dt):
        t = T.__new__(T)
        t.t = es.enter_context(nc.psum_tensor("ps_" + name, shape, dt))
        t.b = Buf(name)
        return t
    psT = [psum_tile(esPA, "psT%d" % i, [128, 8, 128], BF16) for i in range(2)]
    pbank = [psum_tile(esPA, "pbank%d" % i, [128, 512], F32) for i in range(6)]

    esA = contextlib.ExitStack()

    def sbA(name, shape, dt):
        t = T.__new__(T)
        t.t = esA.enter_context(nc.sbuf_tensor("sa_" + name, shape, dt))
        t.b = Buf(name)
        return t

    gpre_bc = sbA("gpre_bc", [128, D], F32)
    gq_col = sbA("gq_col", [128, 6], F32)
    gkv_col = sbA("gkv_col", [128, 4], F32)
    bf_bc = sbA("bf_bc", [128, 8], F32)
    cst = sbA("cst", [64, 4], F32)
    wuq = sbA("wuq", [128, 6, 1536], BF16)
    wuqsw = sbA("wuqsw", [128, 6, 512], BF16)
    wuk = sbA("wuk", [128, 4, 1024], BF16)
    wuv = sbA("wuv", [128, 4, 1024], BF16)
    wkr2 = sbA("wkr2", [128, 16, 128], BF16)
    wfl = sbA("wfl", [128, 16, 8], BF16)
    hT = sbA("hT", [128, 16, 1024], BF16)
    NWB = 2
    wbufs = Rot([sbA("wbuf%d" % i, [128, 16, 512], BF16) for i in range(NWB)])
    xs = [sbA("xs%d" % i, [128, D], F32) for i in range(2)]
    hb = [sbA("hb%d" % i, [128, D], BF16) for i in range(2)]
    ssx = sbA("ssx", [128, 32], F32)
    rsx = sbA("rsx", [128, 32], F32)
    latq = sbA("latq", [128, 6, 1024], BF16)
    latkv = sbA("latkv", [128, 4, 1024], BF16)
    sqt = Rot([sbA("sq%d" % i, [128, 512], BF16) for i in range(2)])
    rstd_bc = sbA("rstd_bc", [128, 1024], F32)
    posi = sbA("posi", [64, 512], I32)
    posf = sbA("posf", [64, 512], F32)
    ru = sbA("ru", [64, 512], F32)
    rkf = sbA("rkf", [64, 512], F32)
    C2 = sbA("C2", [64, 1024], F32)
    S2 = sbA("S2", [64, 1024], F32)
    ropeA = Rot([sbA("ropeA%d" % i, [64, 512], F32) for i in range(1)])
    ropeB = Rot([sbA("ropeB%d" % i, [64, 512], F32) for i in range(1)])
    stg = Rot([sbA("stg%d" % i, [128, 512], BF16) for i in range(4)])
    stg_sem = [P.dma_sem("stg%d" % i) for i in range(4)]
    vstM = sbA("vst", [128, 8, 512], BF16)
    vstF = vstM
    zt = sbA("zt", [128, 8], F32)
    zt2 = sbA("zt2", [128, 8], F32)
    zt3 = sbA("zt3", [128, 8], F32)

    banks = Rot(pbank[0:4])
    banks6 = Rot(pbank[0:6])
    ssb = [pbank[4], pbank[5]]

    ldw = P.dma_sem("ldw_res")
    op("sync", lambda e: e.dma_start(out=gpre_bc.t[:], in_=gpre_d.partition_broadcast(128)), writes=[gpre_bc], dma=dsem(gpre_bc))
    op("sync", lambda e: e.dma_start(out=gq_col.t[:], in_=gq_d), writes=[gq_col], dma=dsem(gq_col))
    op("sync", lambda e: e.dma_start(out=gkv_col.t[:], in_=gkv_d), writes=[gkv_col], dma=dsem(gkv_col))
    op("sync", lambda e: e.dma_start(out=bf_bc.t[:], in_=bf_d.partition_broadcast(128)), writes=[bf_bc], dma=dsem(bf_bc))
    op("sync", lambda e: e.dma_start(out=cst.t[:], in_=cst_d), writes=[cst], dma=dsem(cst))
    op("gpsimd", lambda e: e.memset(ssx.t[:], 0.0), writes=[ssx])
    op("gpsimd", lambda e: e.dma_start(out=wkr2.t[:], in_=w_kr2_d.rearrange("(kc p) n -> p kc n", p=128)), writes=[wkr2], dma=dsem(wkr2))
    op("gpsimd", lambda e: e.dma_start(out=wfl.t[:], in_=w_in_v[:, :, C_FLOG:C_FLOG + 8]), writes=[wfl], dma=dsem(wfl))

    def load_resident_weights():
        op("gpsimd", lambda e: e.dma_start(out=wuq.t[:], in_=w_uq_d.rearrange("(kc p) n -> p kc n", p=128)), writes=[wuq], dma=dsem(wuq))
        op("gpsimd", lambda e: e.dma_start(out=wuqsw.t[:], in_=w_uqsw_d.rearrange("(kc p) n -> p kc n", p=128)), writes=[wuqsw], dma=dsem(wuqsw))
        op("gpsimd", lambda e: e.dma_start(out=wuk.t[:], in_=w_uk_d.rearrange("(kc p) n -> p kc n", p=128)), writes=[wuk], dma=dsem(wuk))
        op("gpsimd", lambda e: e.dma_start(out=wuv.t[:], in_=w_uv_d.rearrange("(kc p) n -> p kc n", p=128)), writes=[wuv], dma=dsem(wuv))

    x_sem = [P.dma_sem("ldx%d" % i) for i in range(2)]

    def prepL(sg, blk):
        gb = sg * 8 + blk
        bi = gb % 2
        r0 = gb * 128
        op("gpsimd", lambda e: e.dma_start(out=xs[bi].t[:], in_=x_d[r0:r0 + 128, :]), writes=[xs[bi]], dma=x_sem[bi])

    def prepA(sg, blk):
        gb = sg * 8 + blk
        bi = gb % 2
        op("scalar", lambda e: e.activation(out=hb[bi].t[:], in_=xs[bi].t[:], func=AF.Square, accum_out=ssx.t[:, gb:gb + 1]),
           reads=[xs[bi]], writes=[hb[bi], ssx])
        op("vector", lambda e: e.tensor_scalar(out=rsx.t[:, gb:gb + 1], in0=ssx.t[:, gb:gb + 1], scalar1=1.0 / D, scalar2=EPS,
                                               op0=ALU.mult, op1=ALU.add), reads=[ssx], writes=[rsx])
        op("scalar", lambda e: e.activation(out=rsx.t[:, gb:gb + 1], in_=rsx.t[:, gb:gb + 1], func=AF.Sqrt), reads=[rsx], writes=[rsx])
        op("vector", lambda e: e.reciprocal(out=rsx.t[:, gb:gb + 1], in_=rsx.t[:, gb:gb + 1]), reads=[rsx], writes=[rsx])
        op("vector", lambda e: e.scalar_tensor_tensor(out=hb[bi].t[:], in0=xs[bi].t[:], scalar=rsx.t[:, gb:gb + 1], in1=gpre_bc.t[:],
                                                      op0=ALU.mult, op1=ALU.mult), reads=[xs[bi], rsx, gpre_bc], writes=[hb[bi]])

    def prepB(sg, blk):
        gb = sg * 8 + blk
        bi = gb % 2
        for kc in range(16):
            pt = psT[kc // 8]
            op("tensor", lambda e, kc=kc, pt=pt: e.transpose(out=pt.t[:, kc % 8, :], in_=hb[bi].t[:, kc * 128:(kc + 1) * 128],
                                                             identity=ident_bf.t[:]), reads=[hb[bi], ident_bf], writes=[pt])
        op("scalar", lambda e: e.activation(out=hT.t[:, 0:8, blk * 128:(blk + 1) * 128], in_=psT[0].t[:], func=AF.Copy),
           reads=[psT[0]], writes=[hT])
        op("vector", lambda e: e.tensor_copy(out=hT.t[:, 8:16, blk * 128:(blk + 1) * 128], in_=psT[1].t[:]),
           reads=[psT[1]], writes=[hT])

    def rope_tables(sg):
        for hf in range(2):
            t0 = sg * 1024 + hf * 512
            hs = slice(hf * 512, (hf + 1) * 512)
            op("sync", lambda e, t0=t0: e.dma_start(out=posi.t[:], in_=pos_d[0:1, t0:t0 + 512].partition_broadcast(64)), writes=[posi], dma=dsem(posi))
            op("vector", lambda e: e.tensor_copy(out=posf.t[:], in_=posi.t[:]), reads=[posi], writes=[posf])
            for col, tab in ((1, S2), (2, C2)):
                op("vector", lambda e, col=col: e.tensor_scalar(out=ru.t[:], in0=posf.t[:], scalar1=cst.t[:, 0:1], scalar2=cst.t[:, col:col + 1],
                                                                op0=ALU.mult, op1=ALU.add), reads=[posf, cst], writes=[ru])
                op("vector", lambda e: e.tensor_copy(out=posi.t[:], in_=ru.t[:]), reads=[ru], writes=[posi])
                op("vector", lambda e: e.tensor_copy(out=rkf.t[:], in_=posi.t[:]), reads=[posi], writes=[rkf])
                op("vector", lambda e: e.tensor_tensor(out=ru.t[:], in0=ru.t[:], in1=rkf.t[:], op=ALU.subtract), reads=[ru, rkf], writes=[ru])
                op("scalar", lambda e, tab=tab, hs=hs: e.activation(out=tab.t[:, hs], in_=ru.t[:], func=AF.Sin, scale=SIN_SCALE), reads=[ru], writes=[tab])

    def stage_out(fn_evac, eng, reads, dst_ap, rows=128):
        i = stg.i % len(stg.tiles)
        st = stg.next()
        op(eng, lambda e: fn_evac(e, st.t[0:rows, :]), reads=reads, writes=[st])
        op("sync", lambda e: e.dma_start(out=dst_ap, in_=st.t[0:rows, :]), reads=[st], dma=stg_sem[i])

    def load_wgroup(c0, ncols):
        wb = wbufs.next()
        op("gpsimd", lambda e: e.dma_start(out=wb.t[:, :, 0:ncols], in_=w_in_v[:, :, c0:c0 + ncols]), writes=[wb],
           dma=P.dma_sem("ldwb%d" % ((wbufs.i - 1) % NWB)))
        return wb

    def fm_matmul(wt, c0, M, half, nk=16, rhs=None, pool=None):
        rhs = rhs or hT
        bank = (pool or banks).next()
        for kc in range(nk):
            op("tensor", lambda e, kc=kc: e.matmul(bank.t[0:M, :], lhsT=wt.t[:, kc, c0:c0 + M], rhs=rhs.t[:, kc, half * 512:(half + 1) * 512],
                                                   start=(kc == 0), stop=(kc == nk - 1)), reads=[wt, rhs], writes=[bank])
        return bank

    def lat_group(wb, nchunks, lat, chunk0, gcol, first, last_total):
        pend = []

        def flush():
            sq, half, cc = pend.pop(0)
            op("tensor", lambda e: e.matmul(ssb[half].t[:], lhsT=ones_bf.t[:], rhs=sq.t[:],
                                            start=(cc == 0), stop=(cc == last_total - 1)),
               reads=[sq, ones_bf], writes=[ssb[half]])
        for c in range(nchunks):
            cc = chunk0 + c
            for half in range(2):
                bank = fm_matmul(wb, c * 128, 128, half)
                op("scalar", lambda e, bank=bank, cc=cc, half=half: e.activation(
                    out=lat.t[:, cc, half * 512:(half + 1) * 512], in_=bank.t[:], func=AF.Copy, scale=gcol.t[:, cc:cc + 1]),
                   reads=[bank, gcol], writes=[lat])
                sq = sqt.next()
                op("scalar", lambda e, bank=bank, sq=sq: e.activation(out=sq.t[:], in_=bank.t[:], func=AF.Square),
                   reads=[bank], writes=[sq])
                if pend:
                    flush()
                pend.append((sq, half, cc))
        while pend:
            flush()

    def lat_normalize(lat, nch, n):
        for half in range(2):
            op("vector", lambda e, half=half: e.tensor_scalar(out=rstd_bc.t[:, half * 512:(half + 1) * 512], in0=ssb[half].t[:],
                                                              scalar1=1.0 / n, scalar2=EPS, op0=ALU.mult, op1=ALU.add),
               reads=[ssb[half]], writes=[rstd_bc])
        op("scalar", lambda e: e.activation(out=rstd_bc.t[:], in_=rstd_bc.t[:], func=AF.Sqrt), reads=[rstd_bc], writes=[rstd_bc])
        op("vector", lambda e: e.reciprocal(out=rstd_bc.t[:], in_=rstd_bc.t[:]), reads=[rstd_bc], writes=[rstd_bc])
        for c in range(nch):
            op("vector", lambda e, c=c: e.tensor_tensor(out=lat.t[:, c, :], in0=lat.t[:, c, :], in1=rstd_bc.t[:], op=ALU.mult),
               reads=[lat, rstd_bc], writes=[lat])

    def rope_evac(bank_x, bank_xs, half, scale, dst_ap):
        ra = ropeA.next()
        rb = ropeB.next()
        hs = slice(half * 512, (half + 1) * 512)
        op("vector", lambda e: e.tensor_tensor(out=ra.t[:], in0=bank_x.t[0:64, :], in1=C2.t[:, hs], op=ALU.mult),
           reads=[bank_x, C2], writes=[ra])
        op("vector", lambda e: e.tensor_tensor(out=rb.t[:], in0=bank_xs.t[0:64, :], in1=S2.t[:, hs], op=ALU.mult),
           reads=[bank_xs, S2], writes=[rb])
        op("vector", lambda e: e.tensor_tensor(out=ra.t[:], in0=ra.t[:], in1=rb.t[:], op=ALU.add), reads=[ra, rb], writes=[ra])
        stage_out(lambda e, o: e.activation(out=o, in_=ra.t[:], func=AF.Copy, scale=scale), "scalar", [ra], dst_ap, rows=64)

    PRELOADED = []

    def process_sg(sg, nsg_total):
        own = sg < 2
        t0 = sg * 1024
        q0 = sg * 1024
        if sg == 0:
            prepL(0, 0)
            prepL(0, 1)
            prepA(0, 0)
            for blk in range(8):
                if blk + 1 < 8:
                    prepA(0, blk + 1)
                if blk + 2 < 8:
                    prepL(0, blk + 2)
                prepB(0, blk)
        rope_tables(sg)
        tasks = []

        def t_qlat0(wb):
            lat_group(wb, 4, latq, 0, gq_col, True, 6)

        def t_qlat1(wb):
            lat_group(wb, 2, latq, 4, gq_col, False, 6)
            lat_normalize(latq, 6, 768.0)

        def qup_head(h):
            for half in range(2):
                tok = slice(q0 + half * 512, q0 + (half + 1) * 512)
                bank = fm_matmul(wuq, h * 192, 128, half, nk=6, rhs=latq, pool=banks6)
                stage_out(lambda e, o, bank=bank: e.activation(out=o, in_=bank.t[:], func=AF.Copy, scale=SCALE_MLA),
                          "scalar", [bank], QT_d[h, 0:128, tok])
                bx = fm_matmul(wuq, h * 192 + 128, 64, half, nk=6, rhs=latq, pool=banks6)
                bxs = fm_matmul(wuqsw, h * 64, 64, half, nk=6, rhs=latq, pool=banks6)
                rope_evac(bx, bxs, half, SCALE_MLA, QT_d[h, 128:192, tok])

        def t_kvlat(wb):
            lat_group(wb, 4, latkv, 0, gkv_col, True, 4)
            lat_normalize(latkv, 4, 512.0)

        def kup_head(h):
            for half in range(2):
                tok = slice(t0 + half * 512, t0 + (half + 1) * 512)
                bank = fm_matmul(wuk, h * 128, 128, half, nk=4, rhs=latkv, pool=banks6)
                stage_out(lambda e, o, bank=bank: e.tensor_copy(out=o, in_=bank.t[:]), "vector", [bank], KT_d[h, :, tok])

        def vup_piece(i):
            cg = i // 4
            for blk in ((i % 4) * 2, (i % 4) * 2 + 1):
                bank = banks6.next()
                for kc in range(4):
                    op("tensor", lambda e, kc=kc, bank=bank, blk=blk, cg=cg: e.matmul(
                        bank.t[:], lhsT=latkv.t[:, kc, blk * 128:(blk + 1) * 128], rhs=wuv.t[:, kc, cg * 512:(cg + 1) * 512],
                        start=(kc == 0), stop=(kc == 3)), reads=[latkv, wuv], writes=[bank])
                if blk % 2 == 0:
                    op("vector", lambda e, bank=bank, blk=blk: e.tensor_copy(out=vstM.t[:, blk, :], in_=bank.t[:]),
                       reads=[bank], writes=[vstM])
                else:
                    op("scalar", lambda e, bank=bank, blk=blk: e.activation(out=vstM.t[:, blk, :], in_=bank.t[:], func=AF.Copy),
                       reads=[bank], writes=[vstM])
            if i % 4 == 3:
                for h4 in range(4):
                    h = cg * 4 + h4
                    op("sync", lambda e, h=h, h4=h4: e.dma_start(out=VM_d[h, :, sg * 8:(sg + 1) * 8, :], in_=vstM.t[:, :, h4 * 128:(h4 + 1) * 128]),
                       reads=[vstM], dma=P.dma_sem("vstM"))

        def krope_half(half):
            tok = slice(t0 + half * 512, t0 + (half + 1) * 512)
            bx = fm_matmul(wkr2, 0, 64, half)
            bxs = fm_matmul(wkr2, 64, 64, half)
            rope_evac(bx, bxs, half, 1.0, KR_d[:, tok])

        def mk_fm(dst, idx0, tokoff, kindname):
            def fn(wb):
                for c in range(4):
                    for half in range(2):
                        tok = slice(tokoff + half * 512, tokoff + (half + 1) * 512)
                        bank = fm_matmul(wb, c * 128, 128, half)
                        if kindname == "silu":
                            stage_out(lambda e, o, bank=bank: e.activation(out=o, in_=bank.t[:], func=AF.Silu),
                                      "scalar", [bank], dst[idx0 + c, :, tok])
                        elif kindname == "fq":
                            stage_out(lambda e, o, bank=bank: e.activation(out=o, in_=bank.t[:], func=AF.Copy, scale=SCALE_FOX),
                                      "scalar", [bank], dst[idx0 + c, :, tok])
                        else:
                            stage_out(lambda e, o, bank=bank: e.tensor_copy(out=o, in_=bank.t[:]), "vector", [bank], dst[idx0 + c, :, tok])
            return fn

        def mk_fv(cg):
            def fn(wb):
                for blk in range(8):
                    bank = banks.next()
                    for kc in range(16):
                        op("tensor", lambda e, kc=kc, bank=bank, blk=blk: e.matmul(
                            bank.t[:], lhsT=hT.t[:, kc, blk * 128:(blk + 1) * 128], rhs=wb.t[:, kc, :],
                            start=(kc == 0), stop=(kc == 15)), reads=[hT, wb], writes=[bank])
                    if blk % 2 == 0:
                        op("vector", lambda e, bank=bank, blk=blk: e.tensor_copy(out=vstF.t[:, blk, :], in_=bank.t[:]),
                           reads=[bank], writes=[vstF])
                    else:
                        op("scalar", lambda e, bank=bank, blk=blk: e.activation(out=vstF.t[:, blk, :], in_=bank.t[:], func=AF.Copy),
                           reads=[bank], writes=[vstF])
                for h4 in range(4):
                    h = cg * 4 + h4
                    op("sync", lambda e, h=h, h4=h4: e.dma_start(out=VF_d[h, :, sg * 8:(sg + 1) * 8, :], in_=vstF.t[:, :, h4 * 128:(h4 + 1) * 128]),
                       reads=[vstF], dma=P.dma_sem("vstF"))
            return fn

        def t_flogit(wb):
            for blk in range(8):
                gb = sg * 8 + blk
                bank = banks.next()
                for kc in range(16):
                    op("tensor", lambda e, kc=kc, bank=bank, blk=blk: e.matmul(
                        bank.t[:, 0:8], lhsT=hT.t[:, kc, blk * 128:(blk + 1) * 128], rhs=wfl.t[:, kc, :],
                        start=(kc == 0), stop=(kc == 15)), reads=[hT, wfl], writes=[bank])
                op("vector", lambda e, bank=bank: e.tensor_tensor(out=zt.t[:], in0=bank.t[:, 0:8], in1=bf_bc.t[:], op=ALU.add),
                   reads=[bank, bf_bc], writes=[zt])
                op("vector", lambda e: e.tensor_scalar_mul(out=zt2.t[:], in0=zt.t[:], scalar1=-1.0), reads=[zt], writes=[zt2])
                op("vector", lambda e: e.tensor_tensor(out=zt2.t[:], in0=zt2.t[:], in1=zt.t[:], op=ALU.max), reads=[zt, zt2], writes=[zt2])
                op("scalar", lambda e: e.activation(out=zt2.t[:], in_=zt2.t[:], func=AF.Exp, scale=-1.0), reads=[zt2], writes=[zt2])
                op("vector", lambda e: e.tensor_scalar_add(out=zt2.t[:], in0=zt2.t[:], scalar1=1.0), reads=[zt2], writes=[zt2])
                op("scalar", lambda e: e.activation(out=zt2.t[:], in_=zt2.t[:], func=AF.Ln), reads=[zt2], writes=[zt2])
                op("vector", lambda e: e.tensor_scalar_min(out=zt3.t[:], in0=zt.t[:], scalar1=0.0), reads=[zt], writes=[zt3])
                op("vector", lambda e, gb=gb: e.tensor_tensor(out=LOGF.t[:, gb * 8:(gb + 1) * 8], in0=zt3.t[:], in1=zt2.t[:], op=ALU.subtract),
                   reads=[zt3, zt2], writes=[LOGF])

        if own:
            tasks.append((C_QLAT, 512, t_qlat0))
            tasks.append((C_QLAT + 512, 256, t_qlat1))
        tasks.append((C_KVLAT, 512, t_kvlat))
        tasks.append((None, 0, lambda wb: (krope_half(0), krope_half(1))))
        if own:
            tasks.append((C_GMLA, 512, mk_fm(GT_d, 0, q0, "silu")))
            tasks.append((C_GMLA + 512, 512, mk_fm(GT_d, 4, q0, "silu")))
            tasks.append((C_GFOX, 512, mk_fm(GT_d, 8, q0, "silu")))
            tasks.append((C_GFOX + 512, 512, mk_fm(GT_d, 12, q0, "silu")))
            tasks.append((C_FQ, 512, mk_fm(FQT_d, 0, q0, "fq")))
            tasks.append((C_FQ + 512, 512, mk_fm(FQT_d, 4, q0, "fq")))
        tasks.append((C_FK, 512, mk_fm(FKT_d, 0, t0, "copy")))
        tasks.append((C_FK + 512, 512, mk_fm(FKT_d, 4, t0, "copy")))
        tasks.append((C_FV, 512, mk_fv(0)))
        tasks.append((C_FV + 512, 512, mk_fv(1)))
        tasks.append((None, 0, t_flogit))
        loaded = {}
        wl = [i for i, t in enumerate(tasks) if t[0] is not None]
        for k, wb in enumerate(PRELOADED):
            assert (tasks[wl[k]][0], tasks[wl[k]][1]) == wb[0], (tasks[wl[k]][:2], wb[0])
            loaded[wl[k]] = wb[1]
        del PRELOADED[:]

        def ensure(upto):
            for i in wl:
                if i <= upto and i not in loaded:
                    loaded[i] = load_wgroup(tasks[i][0], tasks[i][1])
        for i, t in enumerate(tasks):
            nxt = [j for j in wl if j > i][:NWB - 1]
            ensure(max([i] + nxt))
            if sg == 0 and i == 0:
                load_resident_weights()
            t[2](loaded.get(i))
        nxt_sg = sg + 1 < nsg_total
        if nxt_sg:
            first = [(C_QLAT, 512), (C_QLAT + 512, 256)] if sg + 1 < 2 else [(C_KVLAT, 512), (C_FK, 512)]
            for (c0_, n_) in first[:NWB]:
                PRELOADED.append(((c0_, n_), load_wgroup(c0_, n_)))
            prepL(sg + 1, 0)
            prepL(sg + 1, 1)
            prepA(sg + 1, 0)
        for i in range(8):
            if nxt_sg and i + 1 < 8:
                prepA(sg + 1, i + 1)
            if nxt_sg and i + 2 < 8:
                prepL(sg + 1, i + 2)
            if own:
                qup_head(i)
            kup_head(i)
            vup_piece(i)
            if nxt_sg:
                prepB(sg + 1, i)

    nsg = 4
    if STOP_AFTER.startswith("A"):
        nsg = int(STOP_AFTER[1:])
    for sg in range(nsg):
        process_sg(sg, nsg)

    P.barrier()
    esA.close()
    esPA.close()
    esPB = contextlib.ExitStack()
    pbank = [psum_tile(esPB, "pbB%d" % i, [128, 512], F32) for i in range(8)]

    esB = contextlib.ExitStack()

    def sbB(name, shape, dt):
        t = T.__new__(T)
        t.t = esB.enter_context(nc.sbuf_tensor("sc_" + name, shape, dt))
        t.b = Buf(name)
        return t

    if not STOP_AFTER.startswith("A"):
        mpred = sbB("mpred", [32, 32], F32)
        Tt = sbB("Tt", [32, 8], F32)
        Xp = sbB("Xp", [32, 256], F32)
        op("sync", lambda e: e.dma_start(out=mpred.t[:], in_=mpred_d), writes=[mpred], dma=dsem(mpred))
        bW, bT, bP = pbank[0], pbank[1], pbank[2]
        op("tensor", lambda e: e.matmul(bW.t[:, 0:256], lhsT=utri_f.t[:], rhs=LOGF.t[:], start=True, stop=True),
           reads=[utri_f, LOGF], writes=[bW])
        for h in range(8):
            op("tensor", lambda e, h=h: e.matmul(bT.t[0:32, h:h + 1], lhsT=LOGF.t[:, h:256:8], rhs=ones_f.t[:, 0:1], start=True, stop=True),
               reads=[LOGF, ones_f], writes=[bT])
        op("vector", lambda e: e.tensor_copy(out=Tt.t[:], in_=bT.t[0:32, 0:8]), reads=[bT], writes=[Tt])
        for h in range(8):
            op("vector", lambda e, h=h: e.tensor_scalar(out=Xp.t[:, h:256:8], in0=mpred.t[:], scalar1=Tt.t[:, h:h + 1], scalar2=None, op0=ALU.mult),
               reads=[mpred, Tt], writes=[Xp])
        op("tensor", lambda e: e.matmul(bP.t[:, 0:256], lhsT=ones_f.t[0:32, :], rhs=Xp.t[:], start=True, stop=True),
           reads=[ones_f, Xp], writes=[bP])
        op("vector", lambda e: e.tensor_copy(out=CC.t[:], in_=bW.t[:, 0:256]), reads=[bW], writes=[CC])
        op("vector", lambda e: e.tensor_tensor(out=CC.t[:], in0=CC.t[:], in1=bP.t[:, 0:256], op=ALU.add), reads=[CC, bP], writes=[CC])
        op("vector", lambda e: e.tensor_scalar_mul(out=NEGC.t[:], in0=CC.t[:], scalar1=-1.0), reads=[CC], writes=[NEGC])
        cct = sbB("cct", [128, 128], F32)
        bC = pbank[3]
        op("tensor", lambda e: e.matmul(bC.t[:, 0:128], lhsT=CC.t[:, 0:128], rhs=ident_f.t[:], start=True, stop=True),
           reads=[CC, ident_f], writes=[bC])
        op("vector", lambda e: e.tensor_copy(out=cct.t[:], in_=bC.t[:, 0:128]), reads=[bC], writes=[cct])
        cctv = CCT_d.rearrange("(h s) t -> s h t", s=16)
        for sl in range(16):
            op("sync", lambda e, sl=sl: e.dma_start(out=cctv[sl], in_=cct.t[sl * 8:(sl + 1) * 8, :]), reads=[cct], dma=P.dma_sem("cctst"))
        P.barrier()
        for ks in range(16):
            op("vector", lambda e, ks=ks: e.tensor_scalar(out=NEGCF.t[:, ks * 8:(ks + 1) * 8], in0=NEGC.t[:, (16 + ks) * 8:(17 + ks) * 8],
                                                          scalar1=flag_bc.t[:, ks % 2:ks % 2 + 1], scalar2=None, op0=ALU.add),
               reads=[NEGC, flag_bc], writes=[NEGCF])
        if DEBUG:
            op("sync", lambda e: e.dma_start(out=LOGF_d, in_=LOGF.t[:]), reads=[LOGF], dma=ld_misc)
            op("sync", lambda e: e.dma_start(out=CC_d, in_=CC.t[:]), reads=[CC], dma=ld_misc)

    if not STOP_AFTER:
        wout = sbB("wout", [128, 16, D], BF16)
        gpost_bc = sbB("gpost_bc", [128, D], F32)
        op("sync", lambda e: e.dma_start(out=gpost_bc.t[:], in_=gpost_d.partition_broadcast(128)), writes=[gpost_bc], dma=dsem(gpost_bc))
        OG = [sbB("OG%d" % i, [128, 16, 512], BF16) for i in range(2)]
        kt = [sbB("kt%d" % i, [128, 2, 2048], BF16) for i in range(2)]
        vt = [sbB("vt%d" % i, [128, 2, 16, 128], BF16) for i in range(2)]
        qt = [sbB("qt%d" % i, [128, 512], BF16) for i in range(2)]
        qr = [sbB("qr%d" % i, [128, 512], BF16) for i in range(2)]
        gt = [sbB("gt%d" % i, [128, 512], BF16) for i in range(2)]
        kr = [sbB("kr%d" % i, [128, 2, 2048], BF16) for i in range(1)]
        op("gpsimd", lambda e: e.memset(kr[0].t[64:128, :, :], 0.0), writes=[kr[0]])
        for i in range(2):
            op("gpsimd", lambda e, i=i: e.memset(qr[i].t[64:128, :], 0.0), writes=[qr[i]])
        cqtri = [sbB("cqtri%d" % i, [128, 512], F32) for i in range(2)]
        dacc = [sbB("dacc%d" % i, [128, 512], F32) for i in range(2)]
        kr_sem = [P.dma_sem("ldkr%d" % i) for i in range(1)]
        cqrow = [sbB("cqrow%d" % i, [128, 512], F32) for i in range(2)]
        pt = Rot([sbB("pt%d" % i, [128, 512], BF16) for i in range(4)])
        tmpF = Rot([sbB("tmpF%d" % i, [128, 512], F32) for i in range(3)])
        rden = sbB("rden", [128, 512], F32)
        ysb = sbB("ysb", [128, D], F32)
        ysq = sbB("ysq", [128, 512], BF16)
        xres = [sbB("xres%d" % i, [128, D], F32) for i in range(1)]
        xres_sem = [P.dma_sem("ldxr%d" % i) for i in range(1)]
        out_sem = [P.dma_sem("stout%d" % i) for i in range(1)]
        ssy = sbB("ssy", [128, 4], F32)
        rsy = sbB("rsy", [128, 1], F32)

        for i in range(4):
            op("gpsimd", lambda e, i=i: e.dma_start(out=wout.t[:, i * 4:(i + 1) * 4, :],
                                                    in_=w_out_d.rearrange("(kc p) n -> p kc n", p=128)[:, i * 4:(i + 1) * 4, :]),
               writes=[wout], dma=P.dma_sem("ldwout"))
        Sb = Rot(pbank[0:4])
        ob = [pbank[4], pbank[5]]
        db = [pbank[6], pbank[7]]

        def make_head(g, hh, bi, krb, ogb):
            nk = 4 * g + 4
            mla = hh < 8
            h = hh % 8
            KTs = KT_d if mla else FKT_d
            Vs = VM_d if mla else VF_d
            Qs = QT_d if mla else FQT_d
            ktb, vtb, qtb, qrb, gtb = kt[bi], vt[bi], qt[bi], qr[bi], gt[bi]
            cqb = cqrow[bi]
            cqt = cqtri[bi]
            dac = dacc[bi]
            def loads():
                for kind in range(2):
                    op("sync", lambda e, kind=kind: e.dma_start(out=ktb.t[:, kind, 0:nk * 128], in_=KTs[h, :, kind * 2048:kind * 2048 + nk * 128]),
                       writes=[ktb], dma=dsem(ktb))
                    op("sync", lambda e, kind=kind: e.dma_start(out=vtb.t[:, kind, 0:nk, :], in_=Vs[h, :, kind * 16:kind * 16 + nk, :]),
                       writes=[vtb], dma=dsem(vtb))
                op("sync", lambda e: e.dma_start(out=qtb.t[:], in_=Qs[h, 0:128, g * 512:(g + 1) * 512]), writes=[qtb], dma=dsem(qtb))
                if mla:
                    op("sync", lambda e: e.dma_start(out=qrb.t[0:64, :], in_=QT_d[h, 128:192, g * 512:(g + 1) * 512]), writes=[qrb], dma=dsem(qrb))
                if not mla:
                    r0 = h * 16 + 4 * g
                    op("sync", lambda e: e.dma_start(out=cqb.t[:], in_=CCT_d[r0:r0 + 4, :].rearrange("(o r) t -> o (r t)", o=1).partition_broadcast(128)),
                       writes=[cqb], dma=dsem(cqb))
            def load_gate():
                op("gpsimd", lambda e: e.dma_start(out=gtb.t[:], in_=GT_d[hh, :, g * 512:(g + 1) * 512]), writes=[gtb], dma=dsem(gtb))

            def setup():
                if not mla:
                    for r in range(4):
                        op("vector", lambda e, r=r: e.tensor_tensor(out=cqt.t[:, r * 128:(r + 1) * 128], in0=cqb.t[:, r * 128:(r + 1) * 128],
                                                                    in1=trimask.t[:], op=ALU.add), reads=[cqb, trimask], writes=[cqt])

            fulls, specs = [], []
            for kind in range(2):
                for ks in range(nk):
                    r = ks - 4 * g
                    c0 = 0 if r < 0 else r * 128
                    (specs if r >= 0 else fulls).append((kind, ks, c0, r >= 0))
            if fulls:
                tiles = [fulls.pop(0)]
                step = max(1, len(fulls) // len(specs)) if specs else 1
                fi = 0
                for sp in specs:
                    tiles.extend(fulls[fi:fi + step])
                    fi += step
                    tiles.append(sp)
                tiles.extend(fulls[fi:])
            else:
                tiles = specs
            obank, dbank = ob[bi], db[bi]
            nt = len(tiles)

            def qk(ti):
                kind, ks, c0, special = tiles[ti]
                bank = Sb.next()
                if mla:
                    op("tensor", lambda e: e.matmul(bank.t[:, c0:512], lhsT=ktb.t[:, kind, ks * 128:(ks + 1) * 128], rhs=qtb.t[:, c0:512],
                                                    start=True, stop=False), reads=[ktb, qtb], writes=[bank])
                    op("tensor", lambda e: e.matmul(bank.t[:, c0:512], lhsT=krb.t[:, kind, ks * 128:(ks + 1) * 128], rhs=qrb.t[:, c0:512],
                                                    start=False, stop=True), reads=[krb, qrb], writes=[bank])
                else:
                    op("tensor", lambda e: e.matmul(bank.t[:, c0:512], lhsT=ktb.t[:, kind, ks * 128:(ks + 1) * 128], rhs=qtb.t[:, c0:512],
                                                    start=True, stop=True), reads=[ktb, qtb], writes=[bank])
                return bank

            def softmax_part(ti, bank):
                kind, ks, c0, special = tiles[ti]
                p = pt.next()
                sl = ks
                c1 = c0 + 128
                if mla:
                    op("scalar", lambda e: e.activation(out=p.t[:, c0:512], in_=bank.t[:, c0:512], func=AF.Exp), reads=[bank], writes=[p])
                    if special:
                        if kind == 0:
                            op("vector", lambda e: e.tensor_tensor(out=p.t[:, c0:c1], in0=p.t[:, c0:c1], in1=tri01.t[:], op=ALU.mult),
                               reads=[p, tri01], writes=[p])
                        else:
                            op("vector", lambda e: e.tensor_scalar(out=p.t[:, c0:c1], in0=p.t[:, c0:c1], scalar1=flag01.t[:, sl % 2:sl % 2 + 1],
                                                                   scalar2=None, op0=ALU.mult), reads=[p, flag01], writes=[p])
                    if ti == 0:
                        op("vector", lambda e: e.tensor_copy(out=dac.t[:, c0:512], in_=p.t[:, c0:512]), reads=[p], writes=[dac])
                    else:
                        op("vector", lambda e: e.tensor_tensor(out=dac.t[:, c0:512], in0=dac.t[:, c0:512], in1=p.t[:, c0:512], op=ALU.add),
                           reads=[p, dac], writes=[dac])
                else:
                    tf = tmpF.next()
                    col = (kind * 16 + ks) * 8 + h
                    nc_ = NEGC.t[:, col:col + 1]

                    def add(lo, hi, cq, sc):
                        op("vector", lambda e: e.scalar_tensor_tensor(out=tf.t[:, lo:hi], in0=bank.t[:, lo:hi], scalar=sc, in1=cq.t[:, lo:hi],
                                                                      op0=ALU.add, op1=ALU.add), reads=[bank, cq, NEGC, NEGCF], writes=[tf])
                    if special and kind == 0:
                        add(c0, c1, cqt, nc_)
                        if c1 < 512:
                            add(c1, 512, cqb, nc_)
                    elif special:
                        colf = ks * 8 + h
                        add(c0, c1, cqb, NEGCF.t[:, colf:colf + 1])
                        if c1 < 512:
                            add(c1, 512, cqb, nc_)
                    else:
                        add(c0, 512, cqb, nc_)
                    op("scalar", lambda e: e.activation(out=p.t[:, c0:512], in_=tf.t[:, c0:512], func=AF.Exp), reads=[tf], writes=[p])
                return p

            def pv(ti, p):
                kind, ks, c0, special = tiles[ti]
                op("tensor", lambda e: e.matmul(obank.t[:, c0:512], lhsT=vtb.t[:, kind, ks, :], rhs=p.t[:, c0:512],
                                                start=(ti == 0), stop=(ti == nt - 1)), reads=[vtb, p], writes=[obank])
                if not mla:
                    op("tensor", lambda e: e.matmul(dbank.t[:, c0:512], lhsT=ones_bf.t[:], rhs=p.t[:, c0:512],
                                                    start=(ti == 0), stop=(ti == nt - 1)), reads=[ones_bf, p], writes=[dbank])

            def finish_pe():
                if mla:
                    op("tensor", lambda e: e.matmul(dbank.t[:], lhsT=ones_f.t[:], rhs=dac.t[:], start=True, stop=True),
                       reads=[ones_f, dac], writes=[dbank])

            def finish_act():
                op("scalar", lambda e: e.activation(out=rden.t[:], in_=dbank.t[:], func=AF.Ln), reads=[dbank], writes=[rden])
                op("scalar", lambda e: e.activation(out=rden.t[:], in_=rden.t[:], func=AF.Exp, scale=-1.0), reads=[rden], writes=[rden])

            def finish():
                op("vector", lambda e: e.tensor_tensor(out=rden.t[:], in0=obank.t[:], in1=rden.t[:], op=ALU.mult),
                   reads=[obank, rden], writes=[rden])
                op("gpsimd", lambda e: e.tensor_tensor(out=ogb.t[:, hh, :], in0=rden.t[:], in1=gtb.t[:], op=ALU.mult),
                   reads=[rden, gtb], writes=[ogb])

            class H:
                pass
            H.loads, H.setup, H.qk, H.softmax_part, H.pv, H.finish, H.nt, H.load_gate, H.finish_pe, H.finish_act = loads, setup, qk, softmax_part, pv, finish, nt, load_gate, finish_pe, finish_act
            return H

        def load_kr(g):
            nk = 4 * g + 4
            for kind in range(2):
                op("sync", lambda e, kind=kind: e.dma_start(out=kr[0].t[0:64, kind, 0:nk * 128], in_=KR_d[:, kind * 2048:kind * 2048 + nk * 128]),
                   writes=[kr[0]], dma=kr_sem[0])

        NGROUPS = int(os.environ.get("MK_NGROUPS", "4"))
        ALLH = [[make_head(g, hh, (g * 16 + hh) % 2, kr[0], OG[g % 2]) for hh in range(16)] for g in range(NGROUPS)]

        def do_group(g, prevC):
            nk = 4 * g + 4
            ogb = OG[g % 2]
            heads = ALLH[g]
            nxt_heads = ALLH[g + 1] if g + 1 < NGROUPS else None
            jobs = [(hi, ti) for hi in range(16) for ti in range(heads[hi].nt)]
            LOOK = 3
            pend = []
            state = {"i": 0}

            def issue():
                hi, ti = jobs[state["i"]]
                state["i"] += 1
                H = heads[hi]
                if ti == 0:
                    H.setup()
                pend.append((hi, ti, H.qk(ti)))
            if g == 0:
                load_kr(0)
                heads[0].loads()
                heads[0].load_gate()
                heads[1].loads()
                heads[1].load_gate()
            DEFER = 2
            finq = []

            def defer(n, fn):
                finq.append([n, fn])

            def run_finq(force=False):
                for item in finq:
                    item[0] -= 1
                while finq and (force or finq[0][0] <= 0):
                    finq.pop(0)[1]()
            for _ in range(LOOK):
                issue()
            stride = max(1, len(jobs) // (len(prevC) + 2)) if prevC else 0
            cnt = 0
            while pend:
                hi, ti, bank = pend.pop(0)
                H = heads[hi]
                p = H.softmax_part(ti, bank)
                if state["i"] < len(jobs):
                    issue()
                H.pv(ti, p)
                cnt += 1
                if prevC and cnt % stride == 0:
                    prevC.pop(0)()
                run_finq()
                if ti == H.nt - 4 and hi >= 1:
                    heads[hi - 1].finish_act()
                if ti == H.nt - 1:
                    if hi + 2 < 16:
                        heads[hi + 2].loads()
                    elif nxt_heads is not None:
                        nxt_heads[hi + 2 - 16].loads()
                    if hi == 7 and nxt_heads is not None:
                        load_kr(g + 1)
                    defer(DEFER, H.finish_pe)
                    if hi >= 1:
                        heads[hi - 1].finish()
                        if hi + 1 < 16:
                            heads[hi + 1].load_gate()
                        elif nxt_heads is not None:
                            nxt_heads[0].load_gate()
            run_finq(force=True)
            heads[15].finish_act()
            heads[15].finish()
            if nxt_heads is not None:
                nxt_heads[1].load_gate()
            units = []
            for tb in range(4):
                row0 = (g * 4 + tb) * 128

                def u_start(row0=row0):
                    op("gpsimd", lambda e: e.dma_start(out=xres[0].t[:], in_=x_d[row0:row0 + 128, :]),
                       writes=[xres[0]], dma=xres_sem[0])
                    op("gpsimd", lambda e: e.memset(ssy.t[:], 0.0), writes=[ssy])

                def u_cp(cp, tb=tb, first=False, row0=row0):
                    if first:
                        u_start(row0)
                    bank = Sb.next()
                    for hh in range(16):
                        op("tensor", lambda e, hh=hh: e.matmul(
                            bank.t[:], lhsT=ogb.t[:, hh, tb * 128:(tb + 1) * 128], rhs=wout.t[:, hh, cp * 512:(cp + 1) * 512],
                            start=(hh == 0), stop=(hh == 15)), reads=[ogb, wout], writes=[bank])
                    op("scalar", lambda e: e.activation(out=ysb.t[:, cp * 512:(cp + 1) * 512], in_=bank.t[:], func=AF.Copy),
                       reads=[bank], writes=[ysb])
                    op("scalar", lambda e: e.activation(out=ysq.t[:], in_=bank.t[:], func=AF.Square, accum_out=ssy.t[:, cp:cp + 1]),
                       reads=[bank], writes=[ysq, ssy])

                def u_end(row0=row0):
                    op("vector", lambda e: e.tensor_reduce(out=rsy.t[:], in_=ssy.t[:], axis=mybir.AxisListType.X, op=ALU.add), reads=[ssy], writes=[rsy])
                    op("vector", lambda e: e.tensor_scalar(out=rsy.t[:], in0=rsy.t[:], scalar1=1.0 / D, scalar2=EPS, op0=ALU.mult, op1=ALU.add),
                       reads=[rsy], writes=[rsy])
                    op("scalar", lambda e: e.activation(out=rsy.t[:], in_=rsy.t[:], func=AF.Ln), reads=[rsy], writes=[rsy])
                    op("scalar", lambda e: e.activation(out=rsy.t[:], in_=rsy.t[:], func=AF.Exp, scale=-0.5), reads=[rsy], writes=[rsy])
                    xr = xres[0]
                    op("vector", lambda e: e.scalar_tensor_tensor(out=ysb.t[:], in0=ysb.t[:], scalar=rsy.t[:, 0:1], in1=gpost_bc.t[:],
                                                                  op0=ALU.mult, op1=ALU.mult), reads=[ysb, rsy, gpost_bc], writes=[ysb])
                    op("gpsimd", lambda e: e.tensor_tensor(out=xr.t[:], in0=ysb.t[:], in1=xr.t[:], op=ALU.add),
                       reads=[ysb, xr], writes=[xr])
                    op("gpsimd", lambda e: e.dma_start(out=out_d[row0:row0 + 128, :], in_=xr.t[:]),
                       reads=[xr], dma=out_sem[0])
                for cp in range(4):
                    units.append(lambda cp=cp, u_cp=u_cp: u_cp(cp, first=(cp == 0)))
                units.append(u_end)
            return units

        prevC = []
        for g in range(NGROUPS):
            nxt = do_group(g, prevC)
            while prevC:
                prevC.pop(0)()
            prevC = nxt
        while prevC:
            prevC.pop(0)()

    P.emit()
    esB.close()
    esPB.close()
    return nc


def _perm(half):
    own = [b for b in range(NBLK) if ((b % 4) in (0, 3)) == (half == 0)]
    oth = [b for b in range(NBLK) if b not in own]
    return own, oth


def _host_inputs(core, x, positions, g_pre, w_in, g_q_latent, w_uq, g_kv_latent, w_ukv, b_forget, w_out, g_post, shared):
    b, half = core // 2, core % 2
    own, oth = _perm(half)
    perm = np.array(own + oth)
    xb = np.ascontiguousarray(x[b].reshape(NBLK, 128, D)[perm].reshape(SEQ, D))
    pb = np.ascontiguousarray(positions[b].reshape(NBLK, 128)[perm].reshape(1, SEQ)).astype(np.int32)
    mpred = (perm[:, None] < perm[None, :]).astype(np.float32)
    if half == 0:
        flag = np.array([[NEG, 0.0]], np.float32)
    else:
        flag = np.array([[0.0, NEG]], np.float32)
    m = dict(shared)
    m.update({"x": xb, "pos": pb, "mpred": np.ascontiguousarray(mpred), "flag": flag})
    return m


def _shared_inputs(g_pre, w_in, g_q_latent, w_uq, g_kv_latent, w_ukv, b_forget, w_out, g_post):
    f32 = np.float32
    w_in0 = np.ascontiguousarray(w_in[0], dtype=f32)
    w_uq0 = np.ascontiguousarray(w_uq[0], dtype=f32)
    w_ukv0 = np.asarray(w_ukv[0], dtype=f32)
    sw = np.concatenate([np.arange(32, 64), np.arange(0, 32)])
    uq3 = w_uq0.reshape(768, 8, 192)
    w_uq_sw = np.ascontiguousarray(uq3[:, :, 128:][:, :, sw].reshape(768, 512))
    kr = w_in0[:, C_KROPE:C_KROPE + 64]
    w_kr2 = np.ascontiguousarray(np.concatenate([kr, kr[:, sw]], axis=1))
    ukv4 = w_ukv0.reshape(512, 8, 2, 128)
    w_uk = np.ascontiguousarray(ukv4[:, :, 0, :].reshape(512, 1024))
    w_uv = np.ascontiguousarray(ukv4[:, :, 1, :].reshape(512, 1024))
    inv_freq = (10000.0 ** (-np.arange(0, 64, 2, dtype=np.float64) / 64.0))
    cst = np.zeros((64, 4), np.float64)
    cst[:, 0] = np.concatenate([inv_freq, inv_freq]) / TWO_PI
    cst[:32, 1] = 0.5
    cst[32:, 1] = 0.0
    cst[:, 2] = 0.25
    return {
        "g_pre": np.ascontiguousarray(g_pre[0:1], dtype=f32),
        "g_post": np.ascontiguousarray(g_post[0:1], dtype=f32),
        "gq_col": np.ascontiguousarray(np.asarray(g_q_latent[0], f32).reshape(6, 128).T),
        "gkv_col": np.ascontiguousarray(np.asarray(g_kv_latent[0], f32).reshape(4, 128).T),
        "b_forget": np.ascontiguousarray(b_forget[0:1], dtype=f32),
        "cst": cst.astype(f32),
        "w_in": w_in0, "w_uq": w_uq0, "w_uq_sw": w_uq_sw, "w_uk": w_uk, "w_uv": w_uv, "w_kr2": w_kr2,
        "w_out": np.ascontiguousarray(w_out[0], dtype=f32),
    }


_NC_CACHE = {}


def kernel(x, positions, g_pre, w_in, g_q_latent, w_uq, g_kv_latent, w_ukv, b_forget, w_out, g_post, _cores=None, _raw=False):
    x = np.asarray(x)
    positions = np.asarray(positions)
    shared = _shared_inputs(np.asarray(g_pre), np.asarray(w_in), np.asarray(g_q_latent), np.asarray(w_uq),
                            np.asarray(g_kv_latent), np.asarray(w_ukv), np.asarray(b_forget), np.asarray(w_out), np.asarray(g_post))
    cores = list(range(8)) if _cores is None else _cores
    in_maps = [_host_inputs(c, x, positions, None, None, None, None, None, None, None, None, None, shared) for c in cores]
    if "nc" not in _NC_CACHE:
        _NC_CACHE["nc"] = build_program()
    nc = _NC_CACHE["nc"]
    res = run_bass_kernel_spmd(nc, in_maps, core_ids=list(range(len(cores))))
    if _raw:
        return res
    out = np.zeros((4, SEQ, D), np.float32)
    for i, c in enumerate(cores):
        b, half = c // 2, c % 2
        own, _ = _perm(half)
        o = np.asarray(res.results[i]["out"]).reshape(16, 128, D)
        ob = out[b].reshape(NBLK, 128, D)
        for s, blk in enumerate(own):
            ob[blk] = o[s]
    return out
```

```python
import os
import contextlib
import numpy as np
import concourse.bass as bass
import concourse.mybir as mybir
from concourse.bass_utils import run_bass_kernel_spmd

F32 = mybir.dt.float32
BF16 = mybir.dt.bfloat16
I32 = mybir.dt.int32
AF = mybir.ActivationFunctionType
ALU = mybir.AluOpType

D = 2048
SEQ = 4096
NBLK = 32
D_IN = 6472
EPS = 1e-6
NEG = -30000.0
SCALE_MLA = 192 ** -0.5
SCALE_FOX = 128 ** -0.5
TWO_PI = 2.0 * np.pi
SIN_SCALE = TWO_PI * (1.0 - 1e-6)
C_QLAT, C_KVLAT, C_KROPE, C_GMLA, C_FQ, C_FK, C_FV, C_FLOG, C_GFOX = 0, 768, 1280, 1344, 2368, 3392, 4416, 5440, 5448

DEBUG = bool(int(os.environ.get("MK_DEBUG", "0")))
STOP_AFTER = os.environ.get("MK_STOP", "")


class Buf:
    __slots__ = ("w", "r", "name")

    def __init__(self, name=""):
        self.w = None
        self.r = {}
        self.name = name


class T:
    def __init__(self, nc, name, shape, dt, psum=False):
        if psum:
            self.t = nc.alloc_psum_tensor("ps_" + name, shape, dt)
        else:
            self.t = nc.alloc_sbuf_tensor("sb_" + name, shape, dt)
        self.b = Buf(name)


def _b(x):
    return x.b if isinstance(x, T) else x


class EngQ:
    def __init__(self, name):
        self.name = name
        self.ops = []
        self.waited = {}


class Prog:
    ENGS = ["tensor", "vector", "scalar", "gpsimd", "sync"]
    SEM_LIMIT = 28000

    def __init__(self, nc):
        self.nc = nc
        self.q = {n: EngQ(n) for n in self.ENGS}
        self.semcount = {}
        self.semkeys = []
        self.cur = {}
        self.gen = {}
        for n in self.ENGS:
            self.gen[n] = 0
            self.cur[n] = "E_%s_0" % n
            self._mksem(self.cur[n])

    def _mksem(self, key):
        if key not in self.semcount:
            self.semcount[key] = 0
            self.semkeys.append(key)

    def dma_sem(self, key):
        key = "D_" + key
        self._mksem(key)
        return key

    def op(self, eng, fn, reads=(), writes=(), dma=None):
        q = self.q[eng]
        need = {}
        for b in reads:
            b = _b(b)
            if b.w is not None:
                k, v = b.w
                if need.get(k, 0) < v:
                    need[k] = v
        for b in writes:
            b = _b(b)
            if b.w is not None:
                k, v = b.w
                if need.get(k, 0) < v:
                    need[k] = v
            for k, v in b.r.items():
                if need.get(k, 0) < v:
                    need[k] = v
        own = self.cur[eng]
        for k, v in need.items():
            if eng == "tensor" and dma is None and k.startswith("E_tensor"):
                continue
            if q.waited.get(k, 0) < v:
                q.waited[k] = v
                q.ops.append(("wait", k, v))
        if dma is not None:
            if dma.startswith("D_") is False:
                dma = self.dma_sem(dma)
            if self.semcount[dma] + 16 > self.SEM_LIMIT:
                raise RuntimeError("dma sem overflow " + dma)
            self.semcount[dma] += 16
            ev = (dma, self.semcount[dma])
            q.ops.append(("op", fn, dma, 16))
        else:
            if self.semcount[own] + 1 > self.SEM_LIMIT:
                self.gen[eng] += 1
                own = "E_%s_%d" % (eng, self.gen[eng])
                self.cur[eng] = own
                self._mksem(own)
            self.semcount[own] += 1
            ev = (own, self.semcount[own])
            q.ops.append(("op", fn, own, 1))
        for b in writes:
            b = _b(b)
            b.w = ev
            b.r = {}
        for b in reads:
            b = _b(b)
            if b.r.get(ev[0], 0) < ev[1]:
                b.r[ev[0]] = ev[1]
        return ev

    def barrier(self):
        for n in self.ENGS:
            q = self.q[n]
            for k in self.semkeys:
                v = self.semcount[k]
                if v > 0 and q.waited.get(k, 0) < v:
                    q.waited[k] = v
                    q.ops.append(("wait", k, v))

    def emit(self):
        nc = self.nc
        self.barrier()
        with contextlib.ExitStack() as es:
            sems = {}
            for k in self.semkeys:
                sems[k] = es.enter_context(nc.semaphore(k))
            block = es.enter_context(nc.Block())

            def make(engname):
                def body(e):
                    for item in self.q[engname].ops:
                        if item[0] == "wait":
                            e.wait_ge(sems[item[1]], item[2])
                        else:
                            _, fn, k, inc = item
                            fn(e).then_inc(sems[k], inc)
                return body
            block.tensor(make("tensor"))
            block.vector(make("vector"))
            block.scalar(make("scalar"))
            block.gpsimd(make("gpsimd"))
            block.sync(make("sync"))


class Rot:
    def __init__(self, tiles):
        self.tiles = tiles
        self.i = 0

    def next(self):
        t = self.tiles[self.i % len(self.tiles)]
        self.i += 1
        return t


def build_program():
    nc = bass.Bass("TRN2", target_bir_lowering=False)
    P = Prog(nc)
    op = P.op

    def dsem(t):
        return P.dma_sem('w_' + t.b.name)

    def din(name, shape, dt=F32):
        return nc.dram_tensor(name, shape, dt, kind="ExternalInput").ap()

    def dscr(name, shape, dt=BF16):
        if DEBUG:
            return nc.dram_tensor(name, shape, dt, kind="ExternalOutput").ap()
        return nc.dram_tensor(name, shape, dt).ap()

    x_d = din("x", [SEQ, D])
    pos_d = din("pos", [1, SEQ], I32)
    gpre_d = din("g_pre", [1, D])
    gpost_d = din("g_post", [1, D])
    gq_d = din("gq_col", [128, 6])
    gkv_d = din("gkv_col", [128, 4])
    bf_d = din("b_forget", [1, 8])
    cst_d = din("cst", [64, 4])
    flag_d = din("flag", [1, 2])
    mpred_d = din("mpred", [32, 32])
    w_in_d = din("w_in", [D, D_IN])
    w_uq_d = din("w_uq", [768, 1536])
    w_uqsw_d = din("w_uq_sw", [768, 512])
    w_uk_d = din("w_uk", [512, 1024])
    w_uv_d = din("w_uv", [512, 1024])
    w_kr2_d = din("w_kr2", [D, 128])
    w_out_d = din("w_out", [D, D])
    out_d = nc.dram_tensor("out", [2048, D], F32, kind="ExternalOutput").ap()

    QT_d = dscr("QT", [8, 192, 2048])
    KT_d = dscr("KT", [8, 128, SEQ])
    KR_d = dscr("KR", [64, SEQ])
    VM_d = dscr("VM", [8, 128, NBLK, 128])
    FQT_d = dscr("FQT", [8, 128, 2048])
    FKT_d = dscr("FKT", [8, 128, SEQ])
    VF_d = dscr("VF", [8, 128, NBLK, 128])
    GT_d = dscr("GT", [16, 128, 2048])
    CCT_d = nc.dram_tensor("CCT", [128, 128], F32).ap()
    if DEBUG:
        LOGF_d = dscr("LOGF_dbg", [128, 256], F32)
        CC_d = dscr("CC_dbg", [128, 256], F32)

    w_in_v = w_in_d.rearrange("(kc p) n -> p kc n", p=128)

    ident_f = T(nc, "ident_f", [128, 128], F32)
    ident_bf = T(nc, "ident_bf", [128, 128], BF16)
    ones_bf = T(nc, "ones_bf", [128, 128], BF16)
    ones_f = T(nc, "ones_f", [128, 128], F32)
    utri_f = T(nc, "utri_f", [128, 128], F32)
    trimask = T(nc, "trimask", [128, 128], F32)
    LOGF = T(nc, "LOGF", [128, 256], F32)
    CC = T(nc, "CC", [128, 256], F32)
    NEGC = T(nc, "NEGC", [128, 256], F32)
    NEGCF = T(nc, "NEGCF", [128, 128], F32)
    flag_bc = T(nc, "flag_bc", [128, 2], F32)
    flag01 = T(nc, "flag01", [128, 2], F32)
    tri01 = T(nc, "tri01", [128, 128], BF16)

    ld_misc = P.dma_sem("ld_misc")
    op("gpsimd", lambda e: e.memset(ident_f.t[:], 0.0), writes=[ident_f])
    op("gpsimd", lambda e: e.affine_select(out=ident_f.t[:], in_=ident_f.t[:], pattern=[[-1, 128]],
                                           compare_op=ALU.not_equal, fill=1.0, base=0, channel_multiplier=1),
       reads=[ident_f], writes=[ident_f])
    op("vector", lambda e: e.tensor_copy(out=ident_bf.t[:], in_=ident_f.t[:]), reads=[ident_f], writes=[ident_bf])
    op("gpsimd", lambda e: e.memset(ones_bf.t[:], 1.0), writes=[ones_bf])
    op("gpsimd", lambda e: e.memset(ones_f.t[:], 1.0), writes=[ones_f])
    op("gpsimd", lambda e: e.memset(utri_f.t[:], 1.0), writes=[utri_f])
    op("gpsimd", lambda e: e.affine_select(out=utri_f.t[:], in_=utri_f.t[:], pattern=[[1, 128]],
                                           compare_op=ALU.is_ge, fill=0.0, base=0, channel_multiplier=-1),
       reads=[utri_f], writes=[utri_f])
    op("gpsimd", lambda e: e.memset(trimask.t[:], 0.0), writes=[trimask])
    op("gpsimd", lambda e: e.affine_select(out=trimask.t[:], in_=trimask.t[:], pattern=[[1, 128]],
                                           compare_op=ALU.is_ge, fill=NEG, base=0, channel_multiplier=-1),
       reads=[trimask], writes=[trimask])
    op("sync", lambda e: e.dma_start(out=flag_bc.t[:], in_=flag_d.partition_broadcast(128)), writes=[flag_bc], dma=dsem(flag_bc))
    op("vector", lambda e: e.tensor_scalar(out=flag01.t[:], in0=flag_bc.t[:], scalar1=1.0 / 30000.0, scalar2=1.0, op0=ALU.mult, op1=ALU.add),
       reads=[flag_bc], writes=[flag01])
    op("vector", lambda e: e.tensor_copy(out=tri01.t[:], in_=utri_f.t[:]), reads=[utri_f], writes=[tri01])

    esPA = contextlib.ExitStack()

    def psum_tile(es, name, shape, dt):
        t = T.__new__(T)
        t.t = es.enter_context(nc.psum_tensor("ps_" + name, shape, dt))
        t.b = Buf(name)
        return t
    psT = [psum_tile(esPA, "psT%d" % i, [128, 8, 128], BF16) for i in range(2)]
    pbank = [psum_tile(esPA, "pbank%d" % i, [128, 512], F32) for i in range(6)]

    esA = contextlib.ExitStack()

    def sbA(name, shape, dt):
        t = T.__new__(T)
        t.t = esA.enter_context(nc.sbuf_tensor("sa_" + name, shape, dt))
        t.b = Buf(name)
        return t

    gpre_bc = sbA("gpre_bc", [128, D], F32)
    gq_col = sbA("gq_col", [128, 6], F32)
    gkv_col = sbA("gkv_col", [128, 4], F32)
    bf_bc = sbA("bf_bc", [128, 8], F32)
    cst = sbA("cst", [64, 4], F32)
    wuq = sbA("wuq", [128, 6, 1536], BF16)
    wuqsw = sbA("wuqsw", [128, 6, 512], BF16)
    wuk = sbA("wuk", [128, 4, 1024], BF16)
    wuv = sbA("wuv", [128, 4, 1024], BF16)
    wkr2 = sbA("wkr2", [128, 16, 128], BF16)
    wfl = sbA("wfl", [128, 16, 8], BF16)
    hT = sbA("hT", [128, 16, 1024], BF16)
    NWB = 2
    wbufs = Rot([sbA("wbuf%d" % i, [128, 16, 512], BF16) for i in range(NWB)])
    xs = [sbA("xs%d" % i, [128, D], F32) for i in range(2)]
    hb = [sbA("hb%d" % i, [128, D], BF16) for i in range(2)]
    ssx = sbA("ssx", [128, 32], F32)
    rsx = sbA("rsx", [128, 32], F32)
    latq = sbA("latq", [128, 6, 1024], BF16)
    latkv = sbA("latkv", [128, 4, 1024], BF16)
    sqt = Rot([sbA("sq%d" % i, [128, 512], BF16) for i in range(2)])
    rstd_bc = sbA("rstd_bc", [128, 1024], F32)
    posi = sbA("posi", [64, 512], I32)
    posf = sbA("posf", [64, 512], F32)
    ru = sbA("ru", [64, 512], F32)
    rkf = sbA("rkf", [64, 512], F32)
    C2 = sbA("C2", [64, 1024], F32)
    S2 = sbA("S2", [64, 1024], F32)
    ropeA = Rot([sbA("ropeA%d" % i, [64, 512], F32) for i in range(1)])
    ropeB = Rot([sbA("ropeB%d" % i, [64, 512], F32) for i in range(1)])
    stg = Rot([sbA("stg%d" % i, [128, 512], BF16) for i in range(4)])
    stg_sem = [P.dma_sem("stg%d" % i) for i in range(4)]
    vstM = sbA("vst", [128, 8, 512], BF16)
    vstF = vstM
    zt = sbA("zt", [128, 8], F32)
    zt2 = sbA("zt2", [128, 8], F32)
    zt3 = sbA("zt3", [128, 8], F32)

    banks = Rot(pbank[0:4])
    banks6 = Rot(pbank[0:6])
    ssb = [pbank[4], pbank[5]]

    ldw = P.dma_sem("ldw_res")
    op("sync", lambda e: e.dma_start(out=gpre_bc.t[:], in_=gpre_d.partition_broadcast(128)), writes=[gpre_bc], dma=dsem(gpre_bc))
    op("sync", lambda e: e.dma_start(out=gq_col.t[:], in_=gq_d), writes=[gq_col], dma=dsem(gq_col))
    op("sync", lambda e: e.dma_start(out=gkv_col.t[:], in_=gkv_d), writes=[gkv_col], dma=dsem(gkv_col))
    op("sync", lambda e: e.dma_start(out=bf_bc.t[:], in_=bf_d.partition_broadcast(128)), writes=[bf_bc], dma=dsem(bf_bc))
    op("sync", lambda e: e.dma_start(out=cst.t[:], in_=cst_d), writes=[cst], dma=dsem(cst))
    op("gpsimd", lambda e: e.memset(ssx.t[:], 0.0), writes=[ssx])
    op("gpsimd", lambda e: e.dma_start(out=wkr2.t[:], in_=w_kr2_d.rearrange("(kc p) n -> p kc n", p=128)), writes=[wkr2], dma=dsem(wkr2))
    op("gpsimd", lambda e: e.dma_start(out=wfl.t[:], in_=w_in_v[:, :, C_FLOG:C_FLOG + 8]), writes=[wfl], dma=dsem(wfl))

    def load_resident_weights():
        op("gpsimd", lambda e: e.dma_start(out=wuq.t[:], in_=w_uq_d.rearrange("(kc p) n -> p kc n", p=128)), writes=[wuq], dma=dsem(wuq))
        op("gpsimd", lambda e: e.dma_start(out=wuqsw.t[:], in_=w_uqsw_d.rearrange("(kc p) n -> p kc n", p=128)), writes=[wuqsw], dma=dsem(wuqsw))
        op("gpsimd", lambda e: e.dma_start(out=wuk.t[:], in_=w_uk_d.rearrange("(kc p) n -> p kc n", p=128)), writes=[wuk], dma=dsem(wuk))
        op("gpsimd", lambda e: e.dma_start(out=wuv.t[:], in_=w_uv_d.rearrange("(kc p) n -> p kc n", p=128)), writes=[wuv], dma=dsem(wuv))

    x_sem = [P.dma_sem("ldx%d" % i) for i in range(2)]

    def prepL(sg, blk):
        gb = sg * 8 + blk
        bi = gb % 2
        r0 = gb * 128
        op("gpsimd", lambda e: e.dma_start(out=xs[bi].t[:], in_=x_d[r0:r0 + 128, :]), writes=[xs[bi]], dma=x_sem[bi])

    def prepA(sg, blk):
        gb = sg * 8 + blk
        bi = gb % 2
        op("scalar", lambda e: e.activation(out=hb[bi].t[:], in_=xs[bi].t[:], func=AF.Square, accum_out=ssx.t[:, gb:gb + 1]),
           reads=[xs[bi]], writes=[hb[bi], ssx])
        op("vector", lambda e: e.tensor_scalar(out=rsx.t[:, gb:gb + 1], in0=ssx.t[:, gb:gb + 1], scalar1=1.0 / D, scalar2=EPS,
                                               op0=ALU.mult, op1=ALU.add), reads=[ssx], writes=[rsx])
        op("scalar", lambda e: e.activation(out=rsx.t[:, gb:gb + 1], in_=rsx.t[:, gb:gb + 1], func=AF.Sqrt), reads=[rsx], writes=[rsx])
        op("vector", lambda e: e.reciprocal(out=rsx.t[:, gb:gb + 1], in_=rsx.t[:, gb:gb + 1]), reads=[rsx], writes=[rsx])
        op("vector", lambda e: e.scalar_tensor_tensor(out=hb[bi].t[:], in0=xs[bi].t[:], scalar=rsx.t[:, gb:gb + 1], in1=gpre_bc.t[:],
                                                      op0=ALU.mult, op1=ALU.mult), reads=[xs[bi], rsx, gpre_bc], writes=[hb[bi]])

    def prepB(sg, blk):
        gb = sg * 8 + blk
        bi = gb % 2
        for kc in range(16):
            pt = psT[kc // 8]
            op("tensor", lambda e, kc=kc, pt=pt: e.transpose(out=pt.t[:, kc % 8, :], in_=hb[bi].t[:, kc * 128:(kc + 1) * 128],
                                                             identity=ident_bf.t[:]), reads=[hb[bi], ident_bf], writes=[pt])
        op("scalar", lambda e: e.activation(out=hT.t[:, 0:8, blk * 128:(blk + 1) * 128], in_=psT[0].t[:], func=AF.Copy),
           reads=[psT[0]], writes=[hT])
        op("vector", lambda e: e.tensor_copy(out=hT.t[:, 8:16, blk * 128:(blk + 1) * 128], in_=psT[1].t[:]),
           reads=[psT[1]], writes=[hT])

    def rope_tables(sg):
        for hf in range(2):
            t0 = sg * 1024 + hf * 512
            hs = slice(hf * 512, (hf + 1) * 512)
            op("sync", lambda e, t0=t0: e.dma_start(out=posi.t[:], in_=pos_d[0:1, t0:t0 + 512].partition_broadcast(64)), writes=[posi], dma=dsem(posi))
            op("vector", lambda e: e.tensor_copy(out=posf.t[:], in_=posi.t[:]), reads=[posi], writes=[posf])
            for col, tab in ((1, S2), (2, C2)):
                op("vector", lambda e, col=col: e.tensor_scalar(out=ru.t[:], in0=posf.t[:], scalar1=cst.t[:, 0:1], scalar2=cst.t[:, col:col + 1],
                                                                op0=ALU.mult, op1=ALU.add), reads=[posf, cst], writes=[ru])
                op("vector", lambda e: e.tensor_copy(out=posi.t[:], in_=ru.t[:]), reads=[ru], writes=[posi])
                op("vector", lambda e: e.tensor_copy(out=rkf.t[:], in_=posi.t[:]), reads=[posi], writes=[rkf])
                op("vector", lambda e: e.tensor_tensor(out=ru.t[:], in0=ru.t[:], in1=rkf.t[:], op=ALU.subtract), reads=[ru, rkf], writes=[ru])
                op("scalar", lambda e, tab=tab, hs=hs: e.activation(out=tab.t[:, hs], in_=ru.t[:], func=AF.Sin, scale=SIN_SCALE), reads=[ru], writes=[tab])

    def stage_out(fn_evac, eng, reads, dst_ap, rows=128):
        i = stg.i % len(stg.tiles)
        st = stg.next()
        op(eng, lambda e: fn_evac(e, st.t[0:rows, :]), reads=reads, writes=[st])
        op("sync", lambda e: e.dma_start(out=dst_ap, in_=st.t[0:rows, :]), reads=[st], dma=stg_sem[i])

    def load_wgroup(c0, ncols):
        wb = wbufs.next()
        op("gpsimd", lambda e: e.dma_start(out=wb.t[:, :, 0:ncols], in_=w_in_v[:, :, c0:c0 + ncols]), writes=[wb],
           dma=P.dma_sem("ldwb%d" % ((wbufs.i - 1) % NWB)))
        return wb

    def fm_matmul(wt, c0, M, half, nk=16, rhs=None, pool=None):
        rhs = rhs or hT
        bank = (pool or banks).next()
        for kc in range(nk):
            op("tensor", lambda e, kc=kc: e.matmul(bank.t[0:M, :], lhsT=wt.t[:, kc, c0:c0 + M], rhs=rhs.t[:, kc, half * 512:(half + 1) * 512],
                                                   start=(kc == 0), stop=(kc == nk - 1)), reads=[wt, rhs], writes=[bank])
        return bank

    def lat_group(wb, nchunks, lat, chunk0, gcol, first, last_total):
        pend = []

        def flush():
            sq, half, cc = pend.pop(0)
            op("tensor", lambda e: e.matmul(ssb[half].t[:], lhsT=ones_bf.t[:], rhs=sq.t[:],
                                            start=(cc == 0), stop=(cc == last_total - 1)),
               reads=[sq, ones_bf], writes=[ssb[half]])
        for c in range(nchunks):
            cc = chunk0 + c
            for half in range(2):
                bank = fm_matmul(wb, c * 128, 128, half)
                op("scalar", lambda e, bank=bank, cc=cc, half=half: e.activation(
                    out=lat.t[:, cc, half * 512:(half + 1) * 512], in_=bank.t[:], func=AF.Copy, scale=gcol.t[:, cc:cc + 1]),
                   reads=[bank, gcol], writes=[lat])
                sq = sqt.next()
                op("scalar", lambda e, bank=bank, sq=sq: e.activation(out=sq.t[:], in_=bank.t[:], func=AF.Square),
                   reads=[bank], writes=[sq])
                if pend:
                    flush()
                pend.append((sq, half, cc))
        while pend:
            flush()

    def lat_normalize(lat, nch, n):
        for half in range(2):
            op("vector", lambda e, half=half: e.tensor_scalar(out=rstd_bc.t[:, half * 512:(half + 1) * 512], in0=ssb[half].t[:],
                                                              scalar1=1.0 / n, scalar2=EPS, op0=ALU.mult, op1=ALU.add),
               reads=[ssb[half]], writes=[rstd_bc])
        op("scalar", lambda e: e.activation(out=rstd_bc.t[:], in_=rstd_bc.t[:], func=AF.Sqrt), reads=[rstd_bc], writes=[rstd_bc])
        op("vector", lambda e: e.reciprocal(out=rstd_bc.t[:], in_=rstd_bc.t[:]), reads=[rstd_bc], writes=[rstd_bc])
        for c in range(nch):
            op("vector", lambda e, c=c: e.tensor_tensor(out=lat.t[:, c, :], in0=lat.t[:, c, :], in1=rstd_bc.t[:], op=ALU.mult),
               reads=[lat, rstd_bc], writes=[lat])

    def rope_evac(bank_x, bank_xs, half, scale, dst_ap):
        ra = ropeA.next()
        rb = ropeB.next()
        hs = slice(half * 512, (half + 1) * 512)
        op("vector", lambda e: e.tensor_tensor(out=ra.t[:], in0=bank_x.t[0:64, :], in1=C2.t[:, hs], op=ALU.mult),
           reads=[bank_x, C2], writes=[ra])
        op("vector", lambda e: e.tensor_tensor(out=rb.t[:], in0=bank_xs.t[0:64, :], in1=S2.t[:, hs], op=ALU.mult),
           reads=[bank_xs, S2], writes=[rb])
        op("vector", lambda e: e.tensor_tensor(out=ra.t[:], in0=ra.t[:], in1=rb.t[:], op=ALU.add), reads=[ra, rb], writes=[ra])
        stage_out(lambda e, o: e.activation(out=o, in_=ra.t[:], func=AF.Copy, scale=scale), "scalar", [ra], dst_ap, rows=64)

    PRELOADED = []

    def process_sg(sg, nsg_total):
        own = sg < 2
        t0 = sg * 1024
        q0 = sg * 1024
        if sg == 0:
            prepL(0, 0)
            prepL(0, 1)
            prepA(0, 0)
            for blk in range(8):
                if blk + 1 < 8:
                    prepA(0, blk + 1)
                if blk + 2 < 8:
                    prepL(0, blk + 2)
                prepB(0, blk)
        rope_tables(sg)
        tasks = []

        def t_qlat0(wb):
            lat_group(wb, 4, latq, 0, gq_col, True, 6)

        def t_qlat1(wb):
            lat_group(wb, 2, latq, 4, gq_col, False, 6)
            lat_normalize(latq, 6, 768.0)

        def qup_head(h):
            for half in range(2):
                tok = slice(q0 + half * 512, q0 + (half + 1) * 512)
                bank = fm_matmul(wuq, h * 192, 128, half, nk=6, rhs=latq, pool=banks6)
                stage_out(lambda e, o, bank=bank: e.activation(out=o, in_=bank.t[:], func=AF.Copy, scale=SCALE_MLA),
                          "scalar", [bank], QT_d[h, 0:128, tok])
                bx = fm_matmul(wuq, h * 192 + 128, 64, half, nk=6, rhs=latq, pool=banks6)
                bxs = fm_matmul(wuqsw, h * 64, 64, half, nk=6, rhs=latq, pool=banks6)
                rope_evac(bx, bxs, half, SCALE_MLA, QT_d[h, 128:192, tok])

        def t_kvlat(wb):
            lat_group(wb, 4, latkv, 0, gkv_col, True, 4)
            lat_normalize(latkv, 4, 512.0)

        def kup_head(h):
            for half in range(2):
                tok = slice(t0 + half * 512, t0 + (half + 1) * 512)
                bank = fm_matmul(wuk, h * 128, 128, half, nk=4, rhs=latkv, pool=banks6)
                stage_out(lambda e, o, bank=bank: e.tensor_copy(out=o, in_=bank.t[:]), "vector", [bank], KT_d[h, :, tok])

        def vup_piece(i):
            cg = i // 4
            for blk in ((i % 4) * 2, (i % 4) * 2 + 1):
                bank = banks6.next()
                for kc in range(4):
                    op("tensor", lambda e, kc=kc, bank=bank, blk=blk, cg=cg: e.matmul(
                        bank.t[:], lhsT=latkv.t[:, kc, blk * 128:(blk + 1) * 128], rhs=wuv.t[:, kc, cg * 512:(cg + 1) * 512],
                        start=(kc == 0), stop=(kc == 3)), reads=[latkv, wuv], writes=[bank])
                if blk % 2 == 0:
                    op("vector", lambda e, bank=bank, blk=blk: e.tensor_copy(out=vstM.t[:, blk, :], in_=bank.t[:]),
                       reads=[bank], writes=[vstM])
                else:
                    op("scalar", lambda e, bank=bank, blk=blk: e.activation(out=vstM.t[:, blk, :], in_=bank.t[:], func=AF.Copy),
                       reads=[bank], writes=[vstM])
            if i % 4 == 3:
                for h4 in range(4):
                    h = cg * 4 + h4
                    op("sync", lambda e, h=h, h4=h4: e.dma_start(out=VM_d[h, :, sg * 8:(sg + 1) * 8, :], in_=vstM.t[:, :, h4 * 128:(h4 + 1) * 128]),
                       reads=[vstM], dma=P.dma_sem("vstM"))

        def krope_half(half):
            tok = slice(t0 + half * 512, t0 + (half + 1) * 512)
            bx = fm_matmul(wkr2, 0, 64, half)
            bxs = fm_matmul(wkr2, 64, 64, half)
            rope_evac(bx, bxs, half, 1.0, KR_d[:, tok])

        def mk_fm(dst, idx0, tokoff, kindname):
            def fn(wb):
                for c in range(4):
                    for half in range(2):
                        tok = slice(tokoff + half * 512, tokoff + (half + 1) * 512)
                        bank = fm_matmul(wb, c * 128, 128, half)
                        if kindname == "silu":
                            stage_out(lambda e, o, bank=bank: e.activation(out=o, in_=bank.t[:], func=AF.Silu),
                                      "scalar", [bank], dst[idx0 + c, :, tok])
                        elif kindname == "fq":
                            stage_out(lambda e, o, bank=bank: e.activation(out=o, in_=bank.t[:], func=AF.Copy, scale=SCALE_FOX),
                                      "scalar", [bank], dst[idx0 + c, :, tok])
                        else:
                            stage_out(lambda e, o, bank=bank: e.tensor_copy(out=o, in_=bank.t[:]), "vector", [bank], dst[idx0 + c, :, tok])
            return fn

        def mk_fv(cg):
            def fn(wb):
                for blk in range(8):
                    bank = banks.next()
                    for kc in range(16):
                        op("tensor", lambda e, kc=kc, bank=bank, blk=blk: e.matmul(
                            bank.t[:], lhsT=hT.t[:, kc, blk * 128:(blk + 1) * 128], rhs=wb.t[:, kc, :],
                            start=(kc == 0), stop=(kc == 15)), reads=[hT, wb], writes=[bank])
                    if blk % 2 == 0:
                        op("vector", lambda e, bank=bank, blk=blk: e.tensor_copy(out=vstF.t[:, blk, :], in_=bank.t[:]),
                           reads=[bank], writes=[vstF])
                    else:
                        op("scalar", lambda e, bank=bank, blk=blk: e.activation(out=vstF.t[:, blk, :], in_=bank.t[:], func=AF.Copy),
                           reads=[bank], writes=[vstF])
                for h4 in range(4):
                    h = cg * 4 + h4
                    op("sync", lambda e, h=h, h4=h4: e.dma_start(out=VF_d[h, :, sg * 8:(sg + 1) * 8, :], in_=vstF.t[:, :, h4 * 128:(h4 + 1) * 128]),
                       reads=[vstF], dma=P.dma_sem("vstF"))
            return fn

        def t_flogit(wb):
            for blk in range(8):
                gb = sg * 8 + blk
                bank = banks.next()
                for kc in range(16):
                    op("tensor", lambda e, kc=kc, bank=bank, blk=blk: e.matmul(
                        bank.t[:, 0:8], lhsT=hT.t[:, kc, blk * 128:(blk + 1) * 128], rhs=wfl.t[:, kc, :],
                        start=(kc == 0), stop=(kc == 15)), reads=[hT, wfl], writes=[bank])
                op("vector", lambda e, bank=bank: e.tensor_tensor(out=zt.t[:], in0=bank.t[:, 0:8], in1=bf_bc.t[:], op=ALU.add),
                   reads=[bank, bf_bc], writes=[zt])
                op("vector", lambda e: e.tensor_scalar_mul(out=zt2.t[:], in0=zt.t[:], scalar1=-1.0), reads=[zt], writes=[zt2])
                op("vector", lambda e: e.tensor_tensor(out=zt2.t[:], in0=zt2.t[:], in1=zt.t[:], op=ALU.max), reads=[zt, zt2], writes=[zt2])
                op("scalar", lambda e: e.activation(out=zt2.t[:], in_=zt2.t[:], func=AF.Exp, scale=-1.0), reads=[zt2], writes=[zt2])
                op("vector", lambda e: e.tensor_scalar_add(out=zt2.t[:], in0=zt2.t[:], scalar1=1.0), reads=[zt2], writes=[zt2])
                op("scalar", lambda e: e.activation(out=zt2.t[:], in_=zt2.t[:], func=AF.Ln), reads=[zt2], writes=[zt2])
                op("vector", lambda e: e.tensor_scalar_min(out=zt3.t[:], in0=zt.t[:], scalar1=0.0), reads=[zt], writes=[zt3])
                op("vector", lambda e, gb=gb: e.tensor_tensor(out=LOGF.t[:, gb * 8:(gb + 1) * 8], in0=zt3.t[:], in1=zt2.t[:], op=ALU.subtract),
                   reads=[zt3, zt2], writes=[LOGF])

        if own:
            tasks.append((C_QLAT, 512, t_qlat0))
            tasks.append((C_QLAT + 512, 256, t_qlat1))
        tasks.append((C_KVLAT, 512, t_kvlat))
        tasks.append((None, 0, lambda wb: (krope_half(0), krope_half(1))))
        if own:
            tasks.append((C_GMLA, 512, mk_fm(GT_d, 0, q0, "silu")))
            tasks.append((C_GMLA + 512, 512, mk_fm(GT_d, 4, q0, "silu")))
            tasks.append((C_GFOX, 512, mk_fm(GT_d, 8, q0, "silu")))
            tasks.append((C_GFOX + 512, 512, mk_fm(GT_d, 12, q0, "silu")))
            tasks.append((C_FQ, 512, mk_fm(FQT_d, 0, q0, "fq")))
            tasks.append((C_FQ + 512, 512, mk_fm(FQT_d, 4, q0, "fq")))
        tasks.append((C_FK, 512, mk_fm(FKT_d, 0, t0, "copy")))
        tasks.append((C_FK + 512, 512, mk_fm(FKT_d, 4, t0, "copy")))
        tasks.append((C_FV, 512, mk_fv(0)))
        tasks.append((C_FV + 512, 512, mk_fv(1)))
        tasks.append((None, 0, t_flogit))
        loaded = {}
        wl = [i for i, t in enumerate(tasks) if t[0] is not None]
        for k, wb in enumerate(PRELOADED):
            assert (tasks[wl[k]][0], tasks[wl[k]][1]) == wb[0], (tasks[wl[k]][:2], wb[0])
            loaded[wl[k]] = wb[1]
        del PRELOADED[:]

        def ensure(upto):
            for i in wl:
                if i <= upto and i not in loaded:
                    loaded[i] = load_wgroup(tasks[i][0], tasks[i][1])
        for i, t in enumerate(tasks):
            nxt = [j for j in wl if j > i][:NWB - 1]
            ensure(max([i] + nxt))
            if sg == 0 and i == 0:
                load_resident_weights()
            t[2](loaded.get(i))
        nxt_sg = sg + 1 < nsg_total
        if nxt_sg:
            prepL(sg + 1, 0)
            prepL(sg + 1, 1)
            first = [(C_QLAT, 512), (C_QLAT + 512, 256)] if sg + 1 < 2 else [(C_KVLAT, 512), (C_FK, 512)]
            for (c0_, n_) in first[:NWB]:
                PRELOADED.append(((c0_, n_), load_wgroup(c0_, n_)))
            prepA(sg + 1, 0)
        for i in range(8):
            if nxt_sg and i + 1 < 8:
                prepA(sg + 1, i + 1)
            if nxt_sg and i + 2 < 8:
                prepL(sg + 1, i + 2)
            if own:
                qup_head(i)
            kup_head(i)
            vup_piece(i)
            if nxt_sg:
                prepB(sg + 1, i)

    nsg = 4
    if STOP_AFTER.startswith("A"):
        nsg = int(STOP_AFTER[1:])
    for sg in range(nsg):
        process_sg(sg, nsg)

    P.barrier()
    esA.close()
    esPA.close()
    esPB = contextlib.ExitStack()
    pbank = [psum_tile(esPB, "pbB%d" % i, [128, 512], F32) for i in range(8)]

    esB = contextlib.ExitStack()

    def sbB(name, shape, dt):
        t = T.__new__(T)
        t.t = esB.enter_context(nc.sbuf_tensor("sc_" + name, shape, dt))
        t.b = Buf(name)
        return t

    if not STOP_AFTER.startswith("A"):
        mpred = sbB("mpred", [32, 32], F32)
        Tt = sbB("Tt", [32, 8], F32)
        Xp = sbB("Xp", [32, 256], F32)
        op("sync", lambda e: e.dma_start(out=mpred.t[:], in_=mpred_d), writes=[mpred], dma=dsem(mpred))
        bW, bT, bP = pbank[0], pbank[1], pbank[2]
        op("tensor", lambda e: e.matmul(bW.t[:, 0:256], lhsT=utri_f.t[:], rhs=LOGF.t[:], start=True, stop=True),
           reads=[utri_f, LOGF], writes=[bW])
        for h in range(8):
            op("tensor", lambda e, h=h: e.matmul(bT.t[0:32, h:h + 1], lhsT=LOGF.t[:, h:256:8], rhs=ones_f.t[:, 0:1], start=True, stop=True),
               reads=[LOGF, ones_f], writes=[bT])
        op("vector", lambda e: e.tensor_copy(out=Tt.t[:], in_=bT.t[0:32, 0:8]), reads=[bT], writes=[Tt])
        for h in range(8):
            op("vector", lambda e, h=h: e.tensor_scalar(out=Xp.t[:, h:256:8], in0=mpred.t[:], scalar1=Tt.t[:, h:h + 1], scalar2=None, op0=ALU.mult),
               reads=[mpred, Tt], writes=[Xp])
        op("tensor", lambda e: e.matmul(bP.t[:, 0:256], lhsT=ones_f.t[0:32, :], rhs=Xp.t[:], start=True, stop=True),
           reads=[ones_f, Xp], writes=[bP])
        op("vector", lambda e: e.tensor_copy(out=CC.t[:], in_=bW.t[:, 0:256]), reads=[bW], writes=[CC])
        op("vector", lambda e: e.tensor_tensor(out=CC.t[:], in0=CC.t[:], in1=bP.t[:, 0:256], op=ALU.add), reads=[CC, bP], writes=[CC])
        op("vector", lambda e: e.tensor_scalar_mul(out=NEGC.t[:], in0=CC.t[:], scalar1=-1.0), reads=[CC], writes=[NEGC])
        cct = sbB("cct", [128, 128], F32)
        bC = pbank[3]
        op("tensor", lambda e: e.matmul(bC.t[:, 0:128], lhsT=CC.t[:, 0:128], rhs=ident_f.t[:], start=True, stop=True),
           reads=[CC, ident_f], writes=[bC])
        op("vector", lambda e: e.tensor_copy(out=cct.t[:], in_=bC.t[:, 0:128]), reads=[bC], writes=[cct])
        cctv = CCT_d.rearrange("(h s) t -> s h t", s=16)
        for sl in range(16):
            op("sync", lambda e, sl=sl: e.dma_start(out=cctv[sl], in_=cct.t[sl * 8:(sl + 1) * 8, :]), reads=[cct], dma=P.dma_sem("cctst"))
        P.barrier()
        for ks in range(16):
            op("vector", lambda e, ks=ks: e.tensor_scalar(out=NEGCF.t[:, ks * 8:(ks + 1) * 8], in0=NEGC.t[:, (16 + ks) * 8:(17 + ks) * 8],
                                                          scalar1=flag_bc.t[:, ks % 2:ks % 2 + 1], scalar2=None, op0=ALU.add),
               reads=[NEGC, flag_bc], writes=[NEGCF])
        if DEBUG:
            op("sync", lambda e: e.dma_start(out=LOGF_d, in_=LOGF.t[:]), reads=[LOGF], dma=ld_misc)
            op("sync", lambda e: e.dma_start(out=CC_d, in_=CC.t[:]), reads=[CC], dma=ld_misc)

    if not STOP_AFTER:
        wout = sbB("wout", [128, 16, D], BF16)
        gpost_bc = sbB("gpost_bc", [128, D], F32)
        op("sync", lambda e: e.dma_start(out=gpost_bc.t[:], in_=gpost_d.partition_broadcast(128)), writes=[gpost_bc], dma=dsem(gpost_bc))
        OG = [sbB("OG%d" % i, [128, 16, 512], BF16) for i in range(2)]
        kt = [sbB("kt%d" % i, [128, 2, 2048], BF16) for i in range(2)]
        vt = [sbB("vt%d" % i, [128, 2, 16, 128], BF16) for i in range(2)]
        qt = [sbB("qt%d" % i, [128, 512], BF16) for i in range(2)]
        qr = [sbB("qr%d" % i, [128, 512], BF16) for i in range(2)]
        gt = [sbB("gt%d" % i, [128, 512], BF16) for i in range(2)]
        kr = [sbB("kr%d" % i, [128, 2, 2048], BF16) for i in range(1)]
        op("gpsimd", lambda e: e.memset(kr[0].t[64:128, :, :], 0.0), writes=[kr[0]])
        for i in range(2):
            op("gpsimd", lambda e, i=i: e.memset(qr[i].t[64:128, :], 0.0), writes=[qr[i]])
        cqtri = [sbB("cqtri%d" % i, [128, 512], F32) for i in range(2)]
        dacc = [sbB("dacc%d" % i, [128, 512], F32) for i in range(2)]
        kr_sem = [P.dma_sem("ldkr%d" % i) for i in range(1)]
        cqrow = [sbB("cqrow%d" % i, [128, 512], F32) for i in range(2)]
        pt = Rot([sbB("pt%d" % i, [128, 512], BF16) for i in range(6)])
        tmpF = Rot([sbB("tmpF%d" % i, [128, 512], F32) for i in range(3)])
        rden = sbB("rden", [128, 512], F32)
        ysb = sbB("ysb", [128, D], F32)
        ysq = sbB("ysq", [128, 512], BF16)
        xres = [sbB("xres%d" % i, [128, D], F32) for i in range(1)]
        xres_sem = [P.dma_sem("ldxr%d" % i) for i in range(1)]
        out_sem = [P.dma_sem("stout%d" % i) for i in range(1)]
        ssy = sbB("ssy", [128, 4], F32)
        rsy = sbB("rsy", [128, 1], F32)

        for i in range(4):
            op("gpsimd", lambda e, i=i: e.dma_start(out=wout.t[:, i * 4:(i + 1) * 4, :],
                                                    in_=w_out_d.rearrange("(kc p) n -> p kc n", p=128)[:, i * 4:(i + 1) * 4, :]),
               writes=[wout], dma=P.dma_sem("ldwout"))
        Sb = Rot(pbank[0:4])
        ob = [pbank[4], pbank[5]]
        db = [pbank[6], pbank[7]]

        def make_head(g, hh, bi, krb, ogb):
            nk = 4 * g + 4
            mla = hh < 8
            h = hh % 8
            KTs = KT_d if mla else FKT_d
            Vs = VM_d if mla else VF_d
            Qs = QT_d if mla else FQT_d
            ktb, vtb, qtb, qrb, gtb = kt[bi], vt[bi], qt[bi], qr[bi], gt[bi]
            cqb = cqrow[bi]
            cqt = cqtri[bi]
            dac = dacc[bi]
            def loads():
                for kind in range(2):
                    op("sync", lambda e, kind=kind: e.dma_start(out=ktb.t[:, kind, 0:nk * 128], in_=KTs[h, :, kind * 2048:kind * 2048 + nk * 128]),
                       writes=[ktb], dma=dsem(ktb))
                    op("sync", lambda e, kind=kind: e.dma_start(out=vtb.t[:, kind, 0:nk, :], in_=Vs[h, :, kind * 16:kind * 16 + nk, :]),
                       writes=[vtb], dma=dsem(vtb))
                op("sync", lambda e: e.dma_start(out=qtb.t[:], in_=Qs[h, 0:128, g * 512:(g + 1) * 512]), writes=[qtb], dma=dsem(qtb))
                if mla:
                    op("sync", lambda e: e.dma_start(out=qrb.t[0:64, :], in_=QT_d[h, 128:192, g * 512:(g + 1) * 512]), writes=[qrb], dma=dsem(qrb))
                if not mla:
                    r0 = h * 16 + 4 * g
                    op("sync", lambda e: e.dma_start(out=cqb.t[:], in_=CCT_d[r0:r0 + 4, :].rearrange("(o r) t -> o (r t)", o=1).partition_broadcast(128)),
                       writes=[cqb], dma=dsem(cqb))
            def load_gate():
                op("gpsimd", lambda e: e.dma_start(out=gtb.t[:], in_=GT_d[hh, :, g * 512:(g + 1) * 512]), writes=[gtb], dma=dsem(gtb))

            def setup():
                if not mla:
                    for r in range(4):
                        op("vector", lambda e, r=r: e.tensor_tensor(out=cqt.t[:, r * 128:(r + 1) * 128], in0=cqb.t[:, r * 128:(r + 1) * 128],
                                                                    in1=trimask.t[:], op=ALU.add), reads=[cqb, trimask], writes=[cqt])

            fulls, specs = [], []
            for kind in range(2):
                for ks in range(nk):
                    r = ks - 4 * g
                    c0 = 0 if r < 0 else r * 128
                    (specs if r >= 0 else fulls).append((kind, ks, c0, r >= 0))
            if fulls:
                tiles = [fulls.pop(0)]
                step = max(1, len(fulls) // len(specs)) if specs else 1
                fi = 0
                for sp in specs:
                    tiles.extend(fulls[fi:fi + step])
                    fi += step
                    tiles.append(sp)
                tiles.extend(fulls[fi:])
            else:
                tiles = specs
            obank, dbank = ob[bi], db[bi]
            nt = len(tiles)

            def qk(ti):
                kind, ks, c0, special = tiles[ti]
                bank = Sb.next()
                if mla:
                    op("tensor", lambda e: e.matmul(bank.t[:, c0:512], lhsT=ktb.t[:, kind, ks * 128:(ks + 1) * 128], rhs=qtb.t[:, c0:512],
                                                    start=True, stop=False), reads=[ktb, qtb], writes=[bank])
                    op("tensor", lambda e: e.matmul(bank.t[:, c0:512], lhsT=krb.t[:, kind, ks * 128:(ks + 1) * 128], rhs=qrb.t[:, c0:512],
                                                    start=False, stop=True), reads=[krb, qrb], writes=[bank])
                else:
                    op("tensor", lambda e: e.matmul(bank.t[:, c0:512], lhsT=ktb.t[:, kind, ks * 128:(ks + 1) * 128], rhs=qtb.t[:, c0:512],
                                                    start=True, stop=True), reads=[ktb, qtb], writes=[bank])
                return bank

            def softmax_part(ti, bank):
                kind, ks, c0, special = tiles[ti]
                p = pt.next()
                sl = ks
                c1 = c0 + 128
                if mla:
                    op("scalar", lambda e: e.activation(out=p.t[:, c0:512], in_=bank.t[:, c0:512], func=AF.Exp), reads=[bank], writes=[p])
                    if special:
                        if kind == 0:
                            op("vector", lambda e: e.tensor_tensor(out=p.t[:, c0:c1], in0=p.t[:, c0:c1], in1=tri01.t[:], op=ALU.mult),
                               reads=[p, tri01], writes=[p])
                        else:
                            op("vector", lambda e: e.tensor_scalar(out=p.t[:, c0:c1], in0=p.t[:, c0:c1], scalar1=flag01.t[:, sl % 2:sl % 2 + 1],
                                                                   scalar2=None, op0=ALU.mult), reads=[p, flag01], writes=[p])
                    if ti == 0:
                        op("vector", lambda e: e.tensor_copy(out=dac.t[:, c0:512], in_=p.t[:, c0:512]), reads=[p], writes=[dac])
                    else:
                        op("vector", lambda e: e.tensor_tensor(out=dac.t[:, c0:512], in0=dac.t[:, c0:512], in1=p.t[:, c0:512], op=ALU.add),
                           reads=[p, dac], writes=[dac])
                else:
                    tf = tmpF.next()
                    col = (kind * 16 + ks) * 8 + h
                    nc_ = NEGC.t[:, col:col + 1]

                    def add(lo, hi, cq, sc):
                        op("vector", lambda e: e.scalar_tensor_tensor(out=tf.t[:, lo:hi], in0=bank.t[:, lo:hi], scalar=sc, in1=cq.t[:, lo:hi],
                                                                      op0=ALU.add, op1=ALU.add), reads=[bank, cq, NEGC, NEGCF], writes=[tf])
                    if special and kind == 0:
                        add(c0, c1, cqt, nc_)
                        if c1 < 512:
                            add(c1, 512, cqb, nc_)
                    elif special:
                        colf = ks * 8 + h
                        add(c0, c1, cqb, NEGCF.t[:, colf:colf + 1])
                        if c1 < 512:
                            add(c1, 512, cqb, nc_)
                    else:
                        add(c0, 512, cqb, nc_)
                    op("scalar", lambda e: e.activation(out=p.t[:, c0:512], in_=tf.t[:, c0:512], func=AF.Exp), reads=[tf], writes=[p])
                return p

            def pv(ti, p):
                kind, ks, c0, special = tiles[ti]
                op("tensor", lambda e: e.matmul(obank.t[:, c0:512], lhsT=vtb.t[:, kind, ks, :], rhs=p.t[:, c0:512],
                                                start=(ti == 0), stop=(ti == nt - 1)), reads=[vtb, p], writes=[obank])
                if not mla:
                    op("tensor", lambda e: e.matmul(dbank.t[:, c0:512], lhsT=ones_bf.t[:], rhs=p.t[:, c0:512],
                                                    start=(ti == 0), stop=(ti == nt - 1)), reads=[ones_bf, p], writes=[dbank])

            def finish_pe():
                if mla:
                    op("tensor", lambda e: e.matmul(dbank.t[:], lhsT=ones_f.t[:], rhs=dac.t[:], start=True, stop=True),
                       reads=[ones_f, dac], writes=[dbank])

            def finish_act():
                op("scalar", lambda e: e.activation(out=rden.t[:], in_=dbank.t[:], func=AF.Ln), reads=[dbank], writes=[rden])
                op("scalar", lambda e: e.activation(out=rden.t[:], in_=rden.t[:], func=AF.Exp, scale=-1.0), reads=[rden], writes=[rden])

            def finish():
                op("vector", lambda e: e.tensor_tensor(out=rden.t[:], in0=obank.t[:], in1=rden.t[:], op=ALU.mult),
                   reads=[obank, rden], writes=[rden])
                op("gpsimd", lambda e: e.tensor_tensor(out=ogb.t[:, hh, :], in0=rden.t[:], in1=gtb.t[:], op=ALU.mult),
                   reads=[rden, gtb], writes=[ogb])

            class H:
                pass
            H.loads, H.setup, H.qk, H.softmax_part, H.pv, H.finish, H.nt, H.load_gate, H.finish_pe, H.finish_act = loads, setup, qk, softmax_part, pv, finish, nt, load_gate, finish_pe, finish_act
            return H

        def load_kr(g):
            nk = 4 * g + 4
            for kind in range(2):
                op("sync", lambda e, kind=kind: e.dma_start(out=kr[0].t[0:64, kind, 0:nk * 128], in_=KR_d[:, kind * 2048:kind * 2048 + nk * 128]),
                   writes=[kr[0]], dma=kr_sem[0])

        NGROUPS = int(os.environ.get("MK_NGROUPS", "4"))
        ALLH = [[make_head(g, hh, (g * 16 + hh) % 2, kr[0], OG[g % 2]) for hh in range(16)] for g in range(NGROUPS)]

        def do_group(g, prevC):
            nk = 4 * g + 4
            ogb = OG[g % 2]
            heads = ALLH[g]
            nxt_heads = ALLH[g + 1] if g + 1 < NGROUPS else None
            jobs = [(hi, ti) for hi in range(16) for ti in range(heads[hi].nt)]
            LOOK = 3
            pend = []
            state = {"i": 0}

            def issue():
                hi, ti = jobs[state["i"]]
                state["i"] += 1
                H = heads[hi]
                if ti == 0:
                    H.setup()
                pend.append((hi, ti, H.qk(ti)))
            if g == 0:
                load_kr(0)
                heads[0].loads()
                heads[0].load_gate()
                heads[1].loads()
                heads[1].load_gate()
            DEFER = 2
            finq = []

            def defer(n, fn):
                finq.append([n, fn])

            def run_finq(force=False):
                for item in finq:
                    item[0] -= 1
                while finq and (force or finq[0][0] <= 0):
                    finq.pop(0)[1]()
            for _ in range(LOOK):
                issue()
            stride = max(1, len(jobs) // (len(prevC) + 2)) if prevC else 0
            cnt = 0
            while pend:
                hi, ti, bank = pend.pop(0)
                H = heads[hi]
                p = H.softmax_part(ti, bank)
                if state["i"] < len(jobs):
                    issue()
                H.pv(ti, p)
                cnt += 1
                if prevC and cnt % stride == 0:
                    prevC.pop(0)()
                run_finq()
                if ti == H.nt - 4 and hi >= 1:
                    heads[hi - 1].finish_act()
                if ti == H.nt - 1:
                    if hi + 2 < 16:
                        heads[hi + 2].loads()
                    elif nxt_heads is not None:
                        nxt_heads[hi + 2 - 16].loads()
                    if hi == 7 and nxt_heads is not None:
                        load_kr(g + 1)
                    defer(DEFER, H.finish_pe)
                    if hi >= 1:
                        heads[hi - 1].finish()
                        if hi + 1 < 16:
                            heads[hi + 1].load_gate()
                        elif nxt_heads is not None:
                            nxt_heads[0].load_gate()
            run_finq(force=True)
            heads[15].finish_act()
            heads[15].finish()
            if nxt_heads is not None:
                nxt_heads[1].load_gate()
            units = []
            for tb in range(4):
                row0 = (g * 4 + tb) * 128

                def u_start(row0=row0):
                    op("gpsimd", lambda e: e.dma_start(out=xres[0].t[:], in_=x_d[row0:row0 + 128, :]),
                       writes=[xres[0]], dma=xres_sem[0])
                    op("gpsimd", lambda e: e.memset(ssy.t[:], 0.0), writes=[ssy])

                def u_cp(cp, tb=tb, first=False, row0=row0):
                    if first:
                        u_start(row0)
                    bank = Sb.next()
                    for hh in range(16):
                        op("tensor", lambda e, hh=hh: e.matmul(
                            bank.t[:], lhsT=ogb.t[:, hh, tb * 128:(tb + 1) * 128], rhs=wout.t[:, hh, cp * 512:(cp + 1) * 512],
                            start=(hh == 0), stop=(hh == 15)), reads=[ogb, wout], writes=[bank])
                    op("scalar", lambda e: e.activation(out=ysb.t[:, cp * 512:(cp + 1) * 512], in_=bank.t[:], func=AF.Copy),
                       reads=[bank], writes=[ysb])
                    op("scalar", lambda e: e.activation(out=ysq.t[:], in_=bank.t[:], func=AF.Square, accum_out=ssy.t[:, cp:cp + 1]),
                       reads=[bank], writes=[ysq, ssy])

                def u_end(row0=row0):
                    op("vector", lambda e: e.tensor_reduce(out=rsy.t[:], in_=ssy.t[:], axis=mybir.AxisListType.X, op=ALU.add), reads=[ssy], writes=[rsy])
                    op("vector", lambda e: e.tensor_scalar(out=rsy.t[:], in0=rsy.t[:], scalar1=1.0 / D, scalar2=EPS, op0=ALU.mult, op1=ALU.add),
                       reads=[rsy], writes=[rsy])
                    op("scalar", lambda e: e.activation(out=rsy.t[:], in_=rsy.t[:], func=AF.Ln), reads=[rsy], writes=[rsy])
                    op("scalar", lambda e: e.activation(out=rsy.t[:], in_=rsy.t[:], func=AF.Exp, scale=-0.5), reads=[rsy], writes=[rsy])
                    xr = xres[0]
                    op("vector", lambda e: e.scalar_tensor_tensor(out=ysb.t[:], in0=ysb.t[:], scalar=rsy.t[:, 0:1], in1=gpost_bc.t[:],
                                                                  op0=ALU.mult, op1=ALU.mult), reads=[ysb, rsy, gpost_bc], writes=[ysb])
                    op("gpsimd", lambda e: e.tensor_tensor(out=xr.t[:], in0=ysb.t[:], in1=xr.t[:], op=ALU.add),
                       reads=[ysb, xr], writes=[xr])
                    op("gpsimd", lambda e: e.dma_start(out=out_d[row0:row0 + 128, :], in_=xr.t[:]),
                       reads=[xr], dma=out_sem[0])
                for cp in range(4):
                    units.append(lambda cp=cp, u_cp=u_cp: u_cp(cp, first=(cp == 0)))
                units.append(u_end)
            return units

        prevC = []
        for g in range(NGROUPS):
            nxt = do_group(g, prevC)
            while prevC:
                prevC.pop(0)()
            prevC = nxt
        while prevC:
            prevC.pop(0)()

    P.emit()
    esB.close()
    esPB.close()
    return nc


def _perm(half):
    own = [b for b in range(NBLK) if ((b % 4) in (0, 3)) == (half == 0)]
    oth = [b for b in range(NBLK) if b not in own]
    return own, oth


def _host_inputs(core, x, positions, g_pre, w_in, g_q_latent, w_uq, g_kv_latent, w_ukv, b_forget, w_out, g_post, shared):
    b, half = core // 2, core % 2
    own, oth = _perm(half)
    perm = np.array(own + oth)
    xb = np.ascontiguousarray(x[b].reshape(NBLK, 128, D)[perm].reshape(SEQ, D))
    pb = np.ascontiguousarray(positions[b].reshape(NBLK, 128)[perm].reshape(1, SEQ)).astype(np.int32)
    mpred = (perm[:, None] < perm[None, :]).astype(np.float32)
    if half == 0:
        flag = np.array([[NEG, 0.0]], np.float32)
    else:
        flag = np.array([[0.0, NEG]], np.float32)
    m = dict(shared)
    m.update({"x": xb, "pos": pb, "mpred": np.ascontiguousarray(mpred), "flag": flag})
    return m


def _shared_inputs(g_pre, w_in, g_q_latent, w_uq, g_kv_latent, w_ukv, b_forget, w_out, g_post):
    f32 = np.float32
    w_in0 = np.ascontiguousarray(w_in[0], dtype=f32)
    w_uq0 = np.ascontiguousarray(w_uq[0], dtype=f32)
    w_ukv0 = np.asarray(w_ukv[0], dtype=f32)
    sw = np.concatenate([np.arange(32, 64), np.arange(0, 32)])
    uq3 = w_uq0.reshape(768, 8, 192)
    w_uq_sw = np.ascontiguousarray(uq3[:, :, 128:][:, :, sw].reshape(768, 512))
    kr = w_in0[:, C_KROPE:C_KROPE + 64]
    w_kr2 = np.ascontiguousarray(np.concatenate([kr, kr[:, sw]], axis=1))
    ukv4 = w_ukv0.reshape(512, 8, 2, 128)
    w_uk = np.ascontiguousarray(ukv4[:, :, 0, :].reshape(512, 1024))
    w_uv = np.ascontiguousarray(ukv4[:, :, 1, :].reshape(512, 1024))
    inv_freq = (10000.0 ** (-np.arange(0, 64, 2, dtype=np.float64) / 64.0))
    cst = np.zeros((64, 4), np.float64)
    cst[:, 0] = np.concatenate([inv_freq, inv_freq]) / TWO_PI
    cst[:32, 1] = 0.5
    cst[32:, 1] = 0.0
    cst[:, 2] = 0.25
    return {
        "g_pre": np.ascontiguousarray(g_pre[0:1], dtype=f32),
        "g_post": np.ascontiguousarray(g_post[0:1], dtype=f32),
        "gq_col": np.ascontiguousarray(np.asarray(g_q_latent[0], f32).reshape(6, 128).T),
        "gkv_col": np.ascontiguousarray(np.asarray(g_kv_latent[0], f32).reshape(4, 128).T),
        "b_forget": np.ascontiguousarray(b_forget[0:1], dtype=f32),
        "cst": cst.astype(f32),
        "w_in": w_in0, "w_uq": w_uq0, "w_uq_sw": w_uq_sw, "w_uk": w_uk, "w_uv": w_uv, "w_kr2": w_kr2,
        "w_out": np.ascontiguousarray(w_out[0], dtype=f32),
    }


_NC_CACHE = {}


def kernel(x, positions, g_pre, w_in, g_q_latent, w_uq, g_kv_latent, w_ukv, b_forget, w_out, g_post, _cores=None, _raw=False):
    x = np.asarray(x)
    positions = np.asarray(positions)
    shared = _shared_inputs(np.asarray(g_pre), np.asarray(w_in), np.asarray(g_q_latent), np.asarray(w_uq),
                            np.asarray(g_kv_latent), np.asarray(w_ukv), np.asarray(b_forget), np.asarray(w_out), np.asarray(g_post))
    cores = list(range(8)) if _cores is None else _cores
    in_maps = [_host_inputs(c, x, positions, None, None, None, None, None, None, None, None, None, shared) for c in cores]
    if "nc" not in _NC_CACHE:
        _NC_CACHE["nc"] = build_program()
    nc = _NC_CACHE["nc"]
    res = run_bass_kernel_spmd(nc, in_maps, core_ids=list(range(len(cores))))
    if _raw:
        return res
    out = np.zeros((4, SEQ, D), np.float32)
    for i, c in enumerate(cores):
        b, half = c // 2, c % 2
        own, _ = _perm(half)
        o = np.asarray(res.results[i]["out"]).reshape(16, 128, D)
        ob = out[b].reshape(NBLK, 128, D)
        for s, blk in enumerate(own):
            ob[blk] = o[s]
    return out
```
